# Optimizing a Trainium2 kernel written in Bass

```python
import math
import jax, jax.numpy as jnp
from jax import lax
import numpy as np

D_MODEL = 1024
BATCH = 4
SEQ = 8192
DEPTH = 2

GRID_W = 64
CTX_LEN = 256
MIX_WIDTH = D_MODEL
F_GROUPS = 4
F_CH = MIX_WIDTH // 2 // F_GROUPS
F_WIDTH = F_GROUPS * F_CH
H_DIFF = 4
DH = MIX_WIDTH // 2 // H_DIFF // 2
DV = 2 * DH
QK_WIDTH = H_DIFF * 2 * DH
ATT_WIDTH = H_DIFF * DV
F0 = 0
Q0 = F0 + F_WIDTH
K0 = Q0 + QK_WIDTH
V0 = K0 + QK_WIDTH
G0 = V0 + ATT_WIDTH
EVEN_IN = G0 + MIX_WIDTH
CONV_WIDTH = MIX_WIDTH
CONV_K = 3
ODD_IN = 4 * CONV_WIDTH
N_EVEN = (DEPTH + 1) // 2
N_ODD = DEPTH // 2
ROPE_BASE = 10000.0
Q_BLOCK = 128
EPS = 1e-6
ATTN_SCALE = DH ** -0.5

kernel_name = "hybrid_fourier_diffattn_shortconv_dit"


def rms_norm(x, g):
    x32 = x.astype(jnp.float32)
    y = x32 * lax.rsqrt(jnp.mean(x32 * x32, axis=-1, keepdims=True) + EPS)
    return (y * g.astype(jnp.float32)).astype(x.dtype)


def ada_params(cvec, w, b):
    m = jax.nn.silu(cvec) @ w + b
    return jnp.split(m, 3, axis=-1)


def modulate(x, g, shift, scale):
    return rms_norm(x, g) * (1.0 + scale) + shift


def need_ctx_after(i):
    return any(j % 2 == 0 for j in range(i + 1, DEPTH))


def lambda_init_fn(layer_idx):
    return 0.8 - 0.6 * math.exp(-0.3 * layer_idx)


def axial_rope_tables(rows):
    pos_r = jnp.repeat(jnp.arange(rows, dtype=jnp.float32), GRID_W)
    pos_c = jnp.tile(jnp.arange(GRID_W, dtype=jnp.float32), rows)
    half = DH // 2
    freqs = ROPE_BASE ** (-jnp.arange(0, half, 2, dtype=jnp.float32) / half)
    ang_r = pos_r[:, None] * freqs
    ang_c = pos_c[:, None] * freqs
    return (jnp.cos(ang_r), jnp.sin(ang_r), jnp.cos(ang_c), jnp.sin(ang_c))


def _rotate(x, cos, sin):
    x1, x2 = jnp.split(x, 2, axis=-1)
    return jnp.concatenate([x1 * cos - x2 * sin, x1 * sin + x2 * cos], axis=-1)


def apply_axial_rope(x, tabs):
    cos_r, sin_r, cos_c, sin_c = [t[None, :, None, None, :].astype(x.dtype) for t in tabs]
    xr, xc = jnp.split(x, 2, axis=-1)
    return jnp.concatenate([_rotate(xr, cos_r, sin_r), _rotate(xc, cos_c, sin_c)], axis=-1)


def fourier_mix(u):
    b, n, _ = u.shape
    u32 = u.reshape(b, n, F_GROUPS, F_CH).astype(jnp.float32)
    f = jnp.fft.fftn(u32, axes=(1, 3), norm="ortho").real
    return f.reshape(b, n, F_WIDTH).astype(u.dtype)


def split_kv(kv, k_g):
    b, n, _ = kv.shape
    k = rms_norm(kv[..., :QK_WIDTH].reshape(b, n, H_DIFF, 2, DH), k_g)
    v = kv[..., QK_WIDTH:].reshape(b, n, H_DIFF, DV)
    return k, v


def split_qkv(proj, q_g, k_g):
    b, n, _ = proj.shape
    q = rms_norm(proj[..., Q0:K0].reshape(b, n, H_DIFF, 2, DH), q_g)
    k, v = split_kv(proj[..., K0:G0], k_g)
    return q, k, v


def diff_attention(q, k, v, lam):
    b, n = q.shape[:2]
    nblk = n // Q_BLOCK
    qb = q.reshape(b, nblk, Q_BLOCK, H_DIFF, 2, DH).swapaxes(0, 1)

    def one_block(qblk):
        s = jnp.einsum('bqhmd,bkhmd->bhmqk', qblk, k,
                       preferred_element_type=jnp.float32) * ATTN_SCALE
        p = jax.nn.softmax(s, axis=-1)
        a = p[:, :, 0] - lam * p[:, :, 1]
        return jnp.einsum('bhqk,bkhv->bqhv', a.astype(v.dtype), v)

    o = lax.map(one_block, qb)
    return o.swapaxes(0, 1).reshape(b, n, H_DIFF, DV)


def even_output(proj, attn_o, subln_g, lam_init, w_out):
    b, n, _ = proj.shape
    fo = fourier_mix(proj[..., F0:Q0])
    ao = (rms_norm(attn_o, subln_g) * (1.0 - lam_init)).reshape(b, n, ATT_WIDTH)
    y = jnp.concatenate([fo, ao], axis=-1) * jax.nn.silu(proj[..., G0:])
    return y @ w_out


def short_conv_mixer(proj, conv_w, w_out):
    bg, cg, xt, g = jnp.split(proj, 4, axis=-1)
    u = cg * xt
    n = u.shape[1]
    pad = CONV_K // 2
    up = jnp.pad(u, ((0, 0), (pad, pad), (0, 0)))
    conv = sum(conv_w[t] * up[:, t:t + n] for t in range(CONV_K))
    return (bg * conv * jax.nn.silu(g)) @ w_out


def setup_inputs(seed: int = 0) -> dict:
    key = jax.random.key(seed)
    ks = jax.random.split(key, 20)
    f32 = jnp.float32
    D = D_MODEL
    nrm = lambda k, shape, s: (jax.random.normal(k, shape, f32) * s)
    return {
        "x": nrm(ks[0], (BATCH, SEQ, D), 1.0),
        "c": nrm(ks[1], (BATCH, D), 1.0),
        "ctx": nrm(ks[2], (BATCH, CTX_LEN, D), 1.0),
        "c_ctx": nrm(ks[3], (D,), 1.0),
        "norm_g": 1.0 + nrm(ks[4], (DEPTH, D), 0.02),
        "ada_w": nrm(ks[5], (DEPTH, D, 3 * D), D ** -0.5),
        "ada_b": nrm(ks[6], (DEPTH, 3 * D), 0.01),
        "even_w_in": nrm(ks[7], (N_EVEN, D, EVEN_IN), D ** -0.5),
        "even_q_norm": 1.0 + nrm(ks[8], (N_EVEN, DH), 0.02),
        "even_k_norm": 1.0 + nrm(ks[9], (N_EVEN, DH), 0.02),
        "even_lambda_q1": nrm(ks[10], (N_EVEN, DH), 0.1),
        "even_lambda_k1": nrm(ks[11], (N_EVEN, DH), 0.1),
        "even_lambda_q2": nrm(ks[12], (N_EVEN, DH), 0.1),
        "even_lambda_k2": nrm(ks[13], (N_EVEN, DH), 0.1),
        "even_subln": 1.0 + nrm(ks[14], (N_EVEN, DV), 0.02),
        "even_w_out": nrm(ks[15], (N_EVEN, MIX_WIDTH, D), MIX_WIDTH ** -0.5),
        "odd_w_in": nrm(ks[16], (N_ODD, D, ODD_IN), D ** -0.5),
        "odd_conv_w": nrm(ks[17], (N_ODD, CONV_K, CONV_WIDTH), CONV_K ** -0.5),
        "odd_w_out": nrm(ks[18], (N_ODD, CONV_WIDTH, D), CONV_WIDTH ** -0.5),
    }


def reference(x, c, ctx, c_ctx, norm_g, ada_w, ada_b, even_w_in, even_q_norm, even_k_norm,
              even_lambda_q1, even_lambda_k1, even_lambda_q2, even_lambda_k2, even_subln,
              even_w_out, odd_w_in, odd_conv_w, odd_w_out):
    n = x.shape[1]
    ROWS = n // GRID_W
    tabs = axial_rope_tables(ROWS)
    x_lat = x
    x_ctx = ctx
    for i in range(DEPTH):
        ctx_out = need_ctx_after(i)
        shift, scale, gate = ada_params(c, ada_w[i], ada_b[i])
        shift_c, scale_c, gate_c = ada_params(c_ctx, ada_w[i], ada_b[i])
        h = modulate(x_lat, norm_g[i], shift[:, None, :], scale[:, None, :])
        if i % 2 == 0:
            e = i // 2
            w_in = even_w_in[e]
            lam_init = lambda_init_fn(i)
            lam = (jnp.exp(jnp.sum(even_lambda_q1[e].astype(jnp.float32) * even_lambda_k1[e].astype(jnp.float32)))
                   - jnp.exp(jnp.sum(even_lambda_q2[e].astype(jnp.float32) * even_lambda_k2[e].astype(jnp.float32)))
                   + lam_init)
            hc = modulate(x_ctx, norm_g[i], shift_c, scale_c)
            if ctx_out:
                proj_c = hc @ w_in
                qc, kc, vc = split_qkv(proj_c, even_q_norm[e], even_k_norm[e])
                oc = diff_attention(qc, kc, vc, lam)
                yc = even_output(proj_c, oc, even_subln[e], lam_init, even_w_out[e])
            else:
                kc, vc = split_kv(hc @ w_in[:, K0:G0], even_k_norm[e])
            proj = h @ w_in
            q, k, v = split_qkv(proj, even_q_norm[e], even_k_norm[e])
            q = apply_axial_rope(q, tabs)
            k = apply_axial_rope(k, tabs)
            k_all = jnp.concatenate([kc, k], axis=1)
            v_all = jnp.concatenate([vc, v], axis=1)
            o = diff_attention(q, k_all, v_all, lam)
            y = even_output(proj, o, even_subln[e], lam_init, even_w_out[e])
        else:
            o_i = i // 2
            y = short_conv_mixer(h @ odd_w_in[o_i], odd_conv_w[o_i], odd_w_out[o_i])
            if ctx_out:
                hc = modulate(x_ctx, norm_g[i], shift_c, scale_c)
                yc = short_conv_mixer(hc @ odd_w_in[o_i], odd_conv_w[o_i], odd_w_out[o_i])
        x_lat = x_lat + gate[:, None, :] * y
        if ctx_out:
            x_ctx = x_ctx + gate_c * yc
    return x_lat
```

```python
import contextlib
import numpy as np
import ml_dtypes
import concourse.bass as bass
import concourse.mybir as mybir
from concourse.bass_utils import run_bass_kernel_spmd

F32 = mybir.dt.float32
BF16 = mybir.dt.bfloat16
ALU = mybir.AluOpType
AF = mybir.ActivationFunctionType
AX = mybir.AxisListType

NTOK = 8192
D = 1024
NCTX = 256
NOWN = 4224
NKEY = NTOK + NCTX
NKT = NKEY // 128
EPS = 1e-6
LAM_INIT = 0.2

ENGS = ("pe", "act", "dve", "pool", "sp")
NDMA_SEM = 8
SEM_CHUNK = 24000


class Instr:
    __slots__ = ("eng", "fn", "deps", "is_dma", "idx", "dma_tok", "signaled", "sigidx")

    def __init__(self, eng, fn, is_dma):
        self.eng = eng
        self.fn = fn
        self.is_dma = is_dma
        self.deps = {}
        self.dma_tok = None
        self.signaled = False
        self.sigidx = -1


class _Rec:
    def __init__(self):
        self.call = None

    def __getattr__(self, name):
        def f(*args, **kw):
            self.call = (name, args, kw)
            return self
        return f


class Sched:
    def __init__(self, nc):
        self.nc = nc
        self.lists = {e: [] for e in ENGS}
        self.res = {}
        self.dma_count = {e: 0 for e in ENGS}

    def _add_dep(self, ins, tok):
        if tok is None:
            return
        key, val = tok
        if key == ins.eng and ins.eng == "pe" and not ins.is_dma:
            return
        if ins.deps.get(key, -1) < val:
            ins.deps[key] = val

    def op(self, eng, fn, reads=(), writes=(), dma=False):
        rec = _Rec()
        fn(rec)
        name, args, kw = rec.call
        ins = Instr(eng, (lambda e, name=name, args=args, kw=kw: getattr(e, name)(*args, **kw)), dma)
        lst = self.lists[eng]
        ins.idx = len(lst)
        writes = list(writes) + [r for r in reads if isinstance(r, tuple) and r[0] == "ps"]
        reads = [r for r in reads if not (isinstance(r, tuple) and r[0] == "ps")] + ["PHASE"]
        if dma:
            j = self.dma_count[eng]
            self.dma_count[eng] = j + 1
            slot = j % NDMA_SEM
            use = j // NDMA_SEM
            key = ("dma", eng, slot)
            if use > 0:
                ins.deps[key] = 16 * use
            tok = (key, 16 * (use + 1))
            ins.dma_tok = tok
        else:
            tok = (eng, ins.idx)
        for r in reads:
            ent = self.res.get(r)
            if ent is not None:
                self._add_dep(ins, ent[0])
        for w in writes:
            ent = self.res.get(w)
            if ent is not None:
                self._add_dep(ins, ent[0])
                for k, v in ent[1].items():
                    self._add_dep(ins, (k, v))
        for r in reads:
            ent = self.res.setdefault(r, [None, {}])
            if ent[1].get(tok[0], -1) < tok[1]:
                ent[1][tok[0]] = tok[1]
        for w in writes:
            self.res[w] = [tok, {}]
        lst.append(ins)
        return ins

    def barrier(self, tiny):
        ins = Instr("pool", lambda e: e.memset(tiny, 0.0), False)
        lst = self.lists["pool"]
        ins.idx = len(lst)
        ent = self.res.get("PHASE")
        if ent is not None:
            self._add_dep(ins, ent[0])
            for k, v in ent[1].items():
                self._add_dep(ins, (k, v))
        self.res["PHASE"] = [("pool", ins.idx), {}]
        lst.append(ins)

    def emit(self):
        nc = self.nc
        for e in ENGS:
            for ins in self.lists[e]:
                for key, val in ins.deps.items():
                    if isinstance(key, str):
                        self.lists[key][val].signaled = True
        nsig = {}
        for e in ENGS:
            c = 0
            for ins in self.lists[e]:
                if not ins.is_dma and ins.signaled:
                    ins.sigidx = c
                    c += 1
            nsig[e] = c
        with contextlib.ExitStack() as st:
            csems = {}
            for e in ENGS:
                n = (nsig[e] + SEM_CHUNK - 1) // SEM_CHUNK
                csems[e] = [st.enter_context(nc.semaphore(f"c_{e}_{i}")) for i in range(max(n, 1))]
            dsems = {}
            for e in ENGS:
                if self.dma_count[e] > 0:
                    for s in range(NDMA_SEM):
                        dsems[("dma", e, s)] = st.enter_context(nc.semaphore(f"d_{e}_{s}"))
            block = st.enter_context(nc.Block())
            lists = self.lists
            dma_count = self.dma_count

            def run(engname, eng):
                waited = {}
                for ins in lists[engname]:
                    for key, val in ins.deps.items():
                        if isinstance(key, str):
                            sidx = lists[key][val].sigidx
                            ch = sidx // SEM_CHUNK
                            v = sidx % SEM_CHUNK + 1
                            if (ch, v) <= waited.get(key, (-1, -1)):
                                continue
                            waited[key] = (ch, v)
                            eng.wait_ge(csems[key][ch], v)
                        else:
                            if waited.get(key, -1) >= val:
                                continue
                            waited[key] = val
                            eng.wait_ge(dsems[key], val)
                    r = ins.fn(eng)
                    if ins.is_dma:
                        r.then_inc(dsems[ins.dma_tok[0]], 16)
                    elif ins.signaled:
                        r.then_inc(csems[engname][ins.sigidx // SEM_CHUNK], 1)
                if engname == "sp":
                    for e in ENGS:
                        n = dma_count[e]
                        for s in range(NDMA_SEM):
                            uses = (n - s + NDMA_SEM - 1) // NDMA_SEM if n > s else 0
                            if uses > 0:
                                eng.wait_ge(dsems[("dma", e, s)], 16 * uses)

            @block.tensor
            def _(eng):
                run("pe", eng)

            @block.scalar
            def _(eng):
                run("act", eng)

            @block.vector
            def _(eng):
                run("dve", eng)

            @block.gpsimd
            def _(eng):
                run("pool", eng)

            @block.sync
            def _(eng):
                run("sp", eng)


class Mem:
    def __init__(self, big, base, limit):
        self.big = big
        self.base = base
        self.off = base
        self.limit = limit

    def reset(self):
        self.off = self.base

    def alloc(self, shape, dt):
        esz = 4 if dt == F32 else 2
        n = int(np.prod(shape)) * esz
        off = (self.off + 63) // 64 * 64
        assert off + n <= self.limit, (off, n, self.limit)
        self.off = off + n
        ap = self.big[:, off // 2:(off + n) // 2]
        if dt == F32:
            ap = ap.bitcast(F32)
        if len(shape) == 2:
            ap = ap.rearrange("p (a b) -> p a b", a=shape[0])
        elif len(shape) == 3:
            ap = ap.rearrange("p (a b c) -> p a b c", a=shape[0], b=shape[1])
        elif len(shape) == 4:
            ap = ap.rearrange("p (a b c d) -> p a b c d", a=shape[0], b=shape[1], c=shape[2])
        return ap


class Rot:
    def __init__(self, items):
        self.items = items
        self.i = 0

    def next(self):
        r = self.items[self.i % len(self.items)]
        self.i += 1
        return r


SBUF_BYTES = 203 * 1024
PERSIST = 16 * 1024


def build(stop_after=None, debug=False):
    nc = bass.Bass("TRN2", target_bir_lowering=False)

    def din(name, shape, dt=F32):
        return nc.dram_tensor(name, list(shape), dt, kind="ExternalInput").ap()

    def dscr(name, shape, dt):
        return nc.dram_tensor(name, list(shape), dt, kind="ExternalOutput" if debug else "Internal").ap()

    x_all = din("x_all", [NTOK, D])
    x_own = din("x_own", [NOWN, D])
    ctx_d = din("ctx", [NCTX, D])
    cc_d = din("cc", [128, 8, 2])
    ada_w = din("ada_w", [2, D, 3 * D])
    ada_b = din("ada_b", [128, 2, 24])
    normg_d = din("norm_g", [128, 2, 8])
    w_in0 = din("w_in0", [D, 3072])
    w_out0 = din("w_out0", [D, D])
    w_in1 = din("w_in1", [D, 4096])
    w_out1 = din("w_out1", [D, D])
    qkg_d = din("qk_g", [2, 64])
    lam_d = din("lam", [256])
    subln_d = din("subln", [128, 1])
    conv_d = din("conv_w", [128, 8, 3])
    ropek_d = din("ropek", [NKEY, 128])
    ropeq_d = din("ropeq", [NOWN, 128])
    t1_d = din("t1", [128, 64, 2, 128], BF16)
    t2_d = din("t2", [128, 66], BF16)
    ccsc_d = din("ccsc", [128, 2, 128], BF16)
    ident_d = din("ident", [128, 128], BF16)
    out_d = nc.dram_tensor("out", [NOWN, D], F32, kind="ExternalOutput").ap()

    KT_scr = dscr("KT_scr", [128, 4, NKEY], BF16)
    V_scr = dscr("V_scr", [NKEY, 512], BF16)
    F_scr = dscr("F_scr", [NTOK, 512], BF16)
    QT_scr = dscr("QT_scr", [128, 4, NOWN], BF16)
    GT_scr = dscr("GT_scr", [128, 8, NOWN], BF16)
    Z_scr = dscr("Z_scr", [2, 64, 128, 512], BF16)
    FO_scr = dscr("FO_scr", [128, 4, NOWN], BF16)
    AO_scr = dscr("AO_scr", [128, 4, NOWN], BF16)
    X1_scr = dscr("X1_scr", [NOWN, D], F32)
    UT_scr = dscr("UT_scr", [128, 8, NOWN + 2], BF16)
    dbg_d = nc.dram_tensor("dbg", [128, 256], F32, kind="ExternalOutput").ap() if debug else None

    with contextlib.ExitStack() as st:
        big = st.enter_context(nc.sbuf_tensor("big", [128, SBUF_BYTES // 2], BF16))
        PSALL = st.enter_context(nc.psum_tensor("psall", [128, 8 * 512], F32))[:]
        PS = [PSALL[:, i * 512:(i + 1) * 512] for i in range(8)]
        PSB = [p.bitcast(BF16) for p in PS]
        S = Sched(nc)
        pm = Mem(big, 0, PERSIST)
        wm = Mem(big, PERSIST, SBUF_BYTES)

        def dma(out, in_, reads=(), writes=(), q="sp", **kw):
            S.op(q, lambda e: e.dma_start(out=out, in_=in_, **kw), reads=reads, writes=writes, dma=True)

        def cast_dma(out, in_, reads=(), writes=()):
            S.op("pool", lambda e: e.dma_start(out=out, in_=in_, max_dma_last_dim=4096), reads=reads, writes=writes, dma=True)

        ident = pm.alloc([128], BF16)
        ones_bf = pm.alloc([128], BF16)
        onesm = pm.alloc([128], BF16)
        ident32 = pm.alloc([128], F32)
        ones32 = pm.alloc([128], F32)
        prm = pm.alloc([6, 8], F32)
        mod = [pm.alloc([24, 2], F32) for _ in range(2)]
        gate_bc = [pm.alloc([1024], F32) for _ in range(2)]
        qg_bc = pm.alloc([64], F32)
        kg_bc = pm.alloc([64], F32)
        neglam = pm.alloc([1], F32)
        subln8 = pm.alloc([1], F32)
        nhalf = pm.alloc([1], F32)
        epsv = pm.alloc([1], F32)
        tiny = pm.alloc([1], F32)
        conv_sb = pm.alloc([8, 3], F32)
        ccsc = pm.alloc([2, 128], BF16)
        t2 = pm.alloc([66], BF16)
        normg = pm.alloc([2, 8], F32)
        adab = pm.alloc([2, 24], F32)

        dma(ident, ident_d, writes=["ident"])
        dma(ccsc, ccsc_d, writes=["ccsc"])
        dma(t2, t2_d, writes=["t2"])
        dma(normg, normg_d, writes=["normg"])
        dma(adab, ada_b, writes=["adab"])
        dma(conv_sb, conv_d, writes=["conv"])
        dma(subln8, subln_d, writes=["subln8"])
        dma(qg_bc, qkg_d[0].partition_broadcast(128), writes=["qg"])
        dma(kg_bc, qkg_d[1].partition_broadcast(128), writes=["kg"])
        S.op("pool", lambda e: e.memset(ones_bf, 1.0), writes=["ones_bf"])
        S.op("pool", lambda e: e.memset(onesm, 1.0 / 128), writes=["onesm"])
        S.op("pool", lambda e: e.memset(ones32, 1.0), writes=["ones32"])
        S.op("pool", lambda e: e.memset(nhalf, -0.5), writes=["nhalf"])
        S.op("pool", lambda e: e.memset(epsv, EPS), writes=["epsv"])
        S.op("dve", lambda e: e.tensor_copy(out=ident32, in_=ident), reads=["ident"], writes=["ident32"])
        S.op("dve", lambda e: e.tensor_scalar(out=subln8, in0=subln8, scalar1=1.0 - LAM_INIT, scalar2=None, op0=ALU.mult),
             reads=["subln8"], writes=["subln8"])

        WTOP = 48 * 1024
        tm = Mem(big, SBUF_BYTES - WTOP, SBUF_BYTES)
        wm.limit = SBUF_BYTES - WTOP
        W0a = tm.alloc([8, 1536], BF16)
        W0b = tm.alloc([8, 1536], BF16)

        def emit_w0_loads():
            for k in range(8):
                cast_dma(W0a[:, k, 0:512], w_in0[k * 128:(k + 1) * 128, 0:512], writes=[("W0aF", k)])
                cast_dma(W0a[:, k, 512:1536], w_in0[k * 128:(k + 1) * 128, 1024:2048], writes=[("W0aKV", k)])
            for k in range(8):
                cast_dma(W0b[:, k, 0:512], w_in0[k * 128:(k + 1) * 128, 512:1024], writes=[("W0bQ", k)])
                cast_dma(W0b[:, k, 512:1536], w_in0[k * 128:(k + 1) * 128, 2048:3072], writes=[("W0bG", k)])

        wm.reset()
        lamb = wm.alloc([256], F32)
        lprod = wm.alloc([256], F32)
        lsum = wm.alloc([4], F32)
        dma(lamb, lam_d.partition_broadcast(128), writes=["lamb"])
        S.op("dve", lambda e: e.tensor_tensor(out=lprod[:, 0:64], in0=lamb[:, 0:64], in1=lamb[:, 64:128], op=ALU.mult),
             reads=["lamb"], writes=["lprod0"])
        S.op("dve", lambda e: e.tensor_tensor(out=lprod[:, 64:128], in0=lamb[:, 128:192], in1=lamb[:, 192:256], op=ALU.mult),
             reads=["lamb"], writes=["lprod1"])
        S.op("dve", lambda e: e.tensor_reduce(out=lsum[:, 0:2], in_=lprod[:, 0:128].rearrange("p (a b) -> p a b", a=2),
                                              axis=AX.X, op=ALU.add), reads=["lprod0", "lprod1"], writes=["lsum"])
        S.op("act", lambda e: e.activation(out=lsum[:, 2:4], in_=lsum[:, 0:2], func=AF.Exp), reads=["lsum"], writes=["lexp"])
        S.op("dve", lambda e: e.tensor_tensor(out=neglam, in0=lsum[:, 3:4], in1=lsum[:, 2:3], op=ALU.subtract),
             reads=["lexp"], writes=["neglam"])
        S.op("dve", lambda e: e.tensor_scalar(out=neglam, in0=neglam, scalar1=-LAM_INIT, scalar2=None, op0=ALU.add),
             reads=["neglam"], writes=["neglam"])

        ccs = wm.alloc([8, 2], F32)
        scs = wm.alloc([8, 2], F32)
        dma(ccs, cc_d, writes=["ccs"])
        S.op("act", lambda e: e.activation(out=scs, in_=ccs, func=AF.Silu), reads=["ccs"], writes=["scs"])
        AW = [wm.alloc([8, 1024], BF16) for _ in range(6)]
        scs_bf = wm.alloc([8, 2], BF16)
        S.op("dve", lambda e: e.tensor_copy(out=scs_bf, in_=scs), reads=["scs"], writes=["scs_bf"])
        diag = wm.alloc([8, 128], F32)
        awi = 0
        for l in range(2):
            psA = PS[l]
            for third in range(3):
                slot = awi
                awi += 1
                cast_dma(AW[slot], ada_w[l][:, third * 1024:(third + 1) * 1024].rearrange("(k p) c -> p k c", p=128),
                         writes=[("AW", slot)])
                for j in range(8):
                    col = (third * 8 + j) * 2
                    for k in range(8):
                        S.op("pe", lambda e, o=psA[:, col:col + 2], a=AW[slot][:, k, j * 128:(j + 1) * 128], b=scs_bf[:, k, :], k=k:
                             e.matmul(o, lhsT=a, rhs=b, start=(k == 0), stop=(k == 7)),
                             reads=[("AW", slot), "scs_bf"], writes=[("ps", l)])
            S.op("dve", lambda e, l=l, psA=psA: e.tensor_tensor(
                out=mod[l], in0=psA[:, 0:48].rearrange("p (a b) -> p a b", b=2),
                in1=adab[:, l, :].unsqueeze(2).to_broadcast([128, 24, 2]), op=ALU.add),
                reads=[("ps", l), "adab"], writes=[("mod", l)])
            for v, (gi, si) in ([(0, (0, 1)), (1, (2, 3))] if l == 0 else [(0, (4, 5))]):
                S.op("dve", lambda e, l=l, v=v, gi=gi: e.scalar_tensor_tensor(
                    out=prm[:, gi, :], in0=mod[l][:, 8:16, v], scalar=1.0, in1=normg[:, l, :], op0=ALU.add, op1=ALU.mult),
                    reads=[("mod", l), "normg"], writes=[("prm", gi)])
                S.op("dve", lambda e, l=l, v=v, si=si: e.tensor_copy(out=prm[:, si, :], in_=mod[l][:, 0:8, v]),
                     reads=[("mod", l)], writes=[("prm", si)])
            for j in range(8):
                S.op("dve", lambda e, l=l, j=j: e.tensor_scalar(out=diag[:, j, :], in0=ident32, scalar1=mod[l][:, 16 + j, 0:1],
                                                                 scalar2=None, op0=ALU.mult),
                     reads=[("mod", l), "ident32"], writes=[("diag", j)])
                S.op("pe", lambda e, l=l, j=j: e.matmul(PS[2 + j // 4][:, (j % 4) * 128:(j % 4 + 1) * 128], lhsT=ones32, rhs=diag[:, j, :],
                                                       start=True, stop=True),
                     reads=[("diag", j), "ones32"], writes=[("ps", 2 + j // 4)])
            for hf in range(2):
                S.op("act", lambda e, l=l, hf=hf: e.activation(out=gate_bc[l][:, hf * 512:(hf + 1) * 512], in_=PS[2 + hf], func=AF.Copy),
                     reads=[("ps", 2 + hf)], writes=[("gate_bc", l, hf)])
        emit_w0_loads()
        if debug:
            dbgt = wm.alloc([256], F32)
            S.op("dve", lambda e: e.tensor_copy(out=dbgt[:, 0:48], in_=prm.rearrange("p a b -> p (a b)")),
                 reads=[("prm", i) for i in range(6)], writes=["dbgt0"])
            S.op("dve", lambda e: e.tensor_copy(out=dbgt[:, 48:49], in_=neglam), reads=["neglam"], writes=["dbgt1"])
            S.op("dve", lambda e: e.tensor_copy(out=dbgt[:, 64:192], in_=gate_bc[1][:, 512:640]), reads=[("gate_bc", 1, 1)], writes=["dbgt2"])
            dma(dbg_d, dbgt, reads=["dbgt0", "dbgt1", "dbgt2"])
        S.barrier(tiny)
        if stop_after == "P0":
            S.emit()
            return nc

        def norm_pre(xin, xres, nt, xs, xsres, junk, ss):
            for t in range(nt):
                S.op("act", lambda e, t=t: e.activation(out=junk[:, t % 2, :], in_=xin[:, t, :], func=AF.Square, accum_out=ss[:, t:t + 1]),
                     reads=xres(t), writes=[("ss", t), ("junk", t % 2)])
            S.op("dve", lambda e: e.tensor_scalar(out=ss[:, 4:4 + nt], in0=ss[:, 0:nt], scalar1=1.0 / D, scalar2=EPS,
                                                  op0=ALU.mult, op1=ALU.add),
                 reads=[("ss", t) for t in range(nt)], writes=["ss_t"])
            S.op("pool", lambda e: e.tensor_tensor(out=ss[:, 8:8 + nt], in0=ss[:, 4:4 + nt], in1=nhalf.to_broadcast([128, nt]), op=ALU.pow),
                 reads=["ss_t", "nhalf"], writes=["rstd"])
            for t in range(nt):
                if t % 2 == 0:
                    S.op("dve", lambda e, t=t: e.tensor_scalar(out=xs[:, t, :], in0=xin[:, t, :], scalar1=ss[:, 8 + t:9 + t], scalar2=None,
                                                               op0=ALU.mult), reads=xres(t) + ["rstd"], writes=[(xsres, t)])
                else:
                    S.op("act", lambda e, t=t: e.activation(out=xs[:, t, :], in_=xin[:, t, :], func=AF.Copy, scale=ss[:, 8 + t:9 + t]),
                         reads=xres(t) + ["rstd"], writes=[(xsres, t)])

        def norm_T(nt, gi, si, hT, hres, xs, xsres):
            n = nt * 128
            for bank in range(4):
                for fc in (2 * bank, 2 * bank + 1):
                    for t in range(nt):
                        S.op("pe", lambda e, fc=fc, t=t, bank=bank: e.transpose(
                            out=PSB[bank][:, (fc % 2) * 512 + t * 128:(fc % 2) * 512 + (t + 1) * 128],
                            in_=xs[:, t, fc * 128:(fc + 1) * 128], identity=ident),
                            reads=[(xsres, t), "ident"], writes=[("ps", bank)])
                for fc in (2 * bank, 2 * bank + 1):
                    if bank % 2 == 0:
                        S.op("act", lambda e, fc=fc, bank=bank: e.activation(
                            out=hT[:, fc, 0:n], in_=PSB[bank][:, (fc % 2) * 512:(fc % 2) * 512 + n], func=AF.Identity,
                            scale=prm[:, gi, fc:fc + 1], bias=prm[:, si, fc:fc + 1]),
                            reads=[("ps", bank), ("prm", gi), ("prm", si)], writes=[(hres, fc)])
                    else:
                        S.op("dve", lambda e, fc=fc, bank=bank: e.tensor_scalar(
                            out=hT[:, fc, 0:n], in0=PSB[bank][:, (fc % 2) * 512:(fc % 2) * 512 + n],
                            scalar1=prm[:, gi, fc:fc + 1], scalar2=prm[:, si, fc:fc + 1], op0=ALU.mult, op1=ALU.add),
                            reads=[("ps", bank), ("prm", gi), ("prm", si)], writes=[(hres, fc)])

        def qk_post(ps, psres, g_bc, gres, rope, roperes, outb, outres, tmp, tr):
            sq, xg, A, B, st8 = tmp
            g3 = lambda ap: ap.rearrange("p (g d) -> p g d", g=8)
            g4 = lambda ap: ap.rearrange("p (g h s) -> p g h s", g=16, h=2)
            S.op("act", lambda e: e.activation(out=sq, in_=ps, func=AF.Square), reads=[psres], writes=[(tr, "sq")])
            S.op("dve", lambda e: e.tensor_tensor(out=g3(xg), in0=g3(ps), in1=g_bc.unsqueeze(1).to_broadcast([128, 8, 64]), op=ALU.mult),
                 reads=[psres, gres], writes=[(tr, "xg")])
            S.op("dve", lambda e: e.tensor_reduce(out=st8[:, 0:8], in_=g3(sq), axis=AX.X, op=ALU.add),
                 reads=[(tr, "sq")], writes=[(tr, "ssg")])
            S.op("dve", lambda e: e.tensor_scalar(out=st8[:, 8:16], in0=st8[:, 0:8], scalar1=1.0 / 64, scalar2=EPS, op0=ALU.mult, op1=ALU.add),
                 reads=[(tr, "ssg")], writes=[(tr, "t8")])
            S.op("pool", lambda e: e.tensor_tensor(out=st8[:, 16:24], in0=st8[:, 8:16], in1=nhalf.to_broadcast([128, 8]), op=ALU.pow),
                 reads=[(tr, "t8"), "nhalf"], writes=[(tr, "r8")])
            TC = rope[:, 0:64]
            TS = rope[:, 64:128]
            S.op("dve", lambda e: e.tensor_tensor(out=g3(A), in0=g3(xg), in1=TC.unsqueeze(1).to_broadcast([128, 8, 64]), op=ALU.mult),
                 reads=[(tr, "xg"), roperes], writes=[(tr, "A")])
            TS4 = TS.rearrange("p (r h s) -> p r h s", r=2, h=2)
            for hh in range(2):
                xv = xg.rearrange("p (g r h s) -> p g r h s", g=8, r=2, h=2)[:, :, :, 1 - hh, :]
                bv = B.rearrange("p (g r h s) -> p g r h s", g=8, r=2, h=2)[:, :, :, hh, :]
                tv = TS4[:, :, hh, :].unsqueeze(1).to_broadcast([128, 8, 2, 16])
                S.op("dve" if hh == 0 else "pool", lambda e, xv=xv, bv=bv, tv=tv: e.tensor_tensor(out=bv, in0=xv, in1=tv, op=ALU.mult),
                     reads=[(tr, "xg"), roperes], writes=[(tr, "B", hh)])
            S.op("pool", lambda e: e.tensor_tensor(out=A, in0=A, in1=B, op=ALU.add),
                 reads=[(tr, "A"), (tr, "B", 0), (tr, "B", 1)], writes=[(tr, "A")])
            S.op("dve", lambda e: e.tensor_tensor(out=g3(outb), in0=g3(A), in1=st8[:, 16:24].unsqueeze(2).to_broadcast([128, 8, 64]), op=ALU.mult),
                 reads=[(tr, "A"), (tr, "r8")], writes=[outres])

        def load_w(dst, src_cols, name):
            for k in range(8):
                cast_dma(dst[:, k, :], src_cols[k * 128:(k + 1) * 128, :], writes=[(name, k)])
            return [(name, k) for k in range(8)]

        wm.reset()
        W0a_res = [("W0aF", k) for k in range(8)] + [("W0aKV", k) for k in range(8)]
        xin_b = [wm.alloc([4, 1024], F32) for _ in range(2)]
        xs_b = [wm.alloc([4, 1024], BF16) for _ in range(2)]
        hT_b = [wm.alloc([8, 512], BF16) for _ in range(2)]
        junk = wm.alloc([2, 1024], BF16)
        ss_b = [wm.alloc([12], F32) for _ in range(2)]
        rope_b = [wm.alloc([4, 128], F32) for _ in range(3)]
        qtmp = [(wm.alloc([512], F32), wm.alloc([512], F32), wm.alloc([512], F32), wm.alloc([512], F32), wm.alloc([24], F32))
                for _ in range(2)]
        kb_b = [wm.alloc([512], BF16) for _ in range(3)]
        KTst = [wm.alloc([4, 512], BF16) for _ in range(2)]
        Fst = [wm.alloc([4, 512], BF16) for _ in range(2)]
        Vst = [wm.alloc([4, 512], BF16) for _ in range(2)]
        pbank = Rot([4, 5, 6, 7, 0, 1, 2, 3])
        qrot = Rot([0, 1])
        kbrot = Rot([0, 1, 2])

        def proj_tok(hT, hres_l, t, W, wcols, wres_l, bank):
            for fc in range(8):
                S.op("pe", lambda e, fc=fc: e.matmul(PS[bank], lhsT=hT[:, fc, t * 128:(t + 1) * 128], rhs=W[:, fc, wcols[0]:wcols[1]],
                                                     start=(fc == 0), stop=(fc == 7)),
                     reads=[hres_l[fc]] + wres_l, writes=[("ps", bank)])

        def qk_tile(bank, g_bc, gres, rope_t, roperes, KT_dst, ktres, t):
            qi = qrot.next()
            ki = kbrot.next()
            kb = kb_b[ki]
            qk_post(PS[bank], ("ps", bank), g_bc, gres, rope_t, roperes, kb, ("kb", ki), qtmp[qi], ("qt", qi))

            def part_b():
                tb = pbank.next()
                for h in range(4):
                    S.op("pe", lambda e, h=h: e.transpose(out=PSB[tb][:, h * 128:(h + 1) * 128], in_=kb[:, h * 128:(h + 1) * 128], identity=ident),
                         reads=[("kb", ki), "ident"], writes=[("ps", tb)])
                S.op("act", lambda e: e.activation(out=KT_dst[:, :, t * 128:(t + 1) * 128],
                                                   in_=PSB[tb][:, 0:512].rearrange("p (h n) -> p h n", h=4), func=AF.Copy),
                     reads=[("ps", tb)], writes=[(ktres, t)])
            return part_b

        nblk_a = 1 + NTOK // 512

        def blk_a(blk):
            if blk == 0:
                return 2, ctx_d.rearrange("(t p) d -> p t d", p=128), 0, 2, 3
            return (4, x_all[(blk - 1) * 512:blk * 512, :].rearrange("(t p) d -> p t d", p=128), NCTX + (blk - 1) * 512, 0, 1)

        def load_a(blk):
            nt, src, key0, gi, si = blk_a(blk)
            sl = blk % 2
            dma(xin_b[sl][:, 0:nt, :], src, writes=[("xin", sl)])
            dma(rope_b[blk % 3][:, 0:nt, :], ropek_d[key0:key0 + nt * 128, :].rearrange("(t p) c -> p t c", p=128), writes=[("rope", blk % 3)])

        xin_res = lambda sl: (lambda t: [("xin", sl)])

        def pre_a(blk):
            nt = blk_a(blk)[0]
            sl = blk % 2
            norm_pre(xin_b[sl], xin_res(sl), nt, xs_b[sl], ("xs", sl), junk, ss_b[sl])

        load_a(0)
        load_a(1)
        pre_a(0)
        deferred = []
        for blk in range(nblk_a):
            sl = blk % 2
            nt, src, key0, gi, si = blk_a(blk)
            n = nt * 128
            if blk + 1 < nblk_a:
                pre_a(blk + 1)
            if blk + 2 < nblk_a:
                load_a(blk + 2)
            if blk == 0:
                norm_T(nt, gi, si, hT_b[sl], ("hT", sl), xs_b[sl], ("xs", sl))
            hres_l = [(("hT", sl), fc) for fc in range(8)]
            for t in range(nt):
                if t == nt - 1 and blk + 1 < nblk_a:
                    ntn, _, _, gin, sin_ = blk_a(blk + 1)
                    norm_T(ntn, gin, sin_, hT_b[1 - sl], ("hT", 1 - sl), xs_b[1 - sl], ("xs", 1 - sl))
                bank = pbank.next()
                proj_tok(hT_b[sl], hres_l, t, W0a, (512, 1024), W0a_res, bank)
                part_b = qk_tile(bank, kg_bc, "kg", rope_b[blk % 3][:, t, :], ("rope", blk % 3), KTst[sl], ("KTst", sl), t)
                bank = pbank.next()
                proj_tok(hT_b[sl], hres_l, t, W0a, (1024, 1536), W0a_res, bank)
                S.op("dve", lambda e, bank=bank, t=t: e.tensor_copy(out=Vst[sl][:, t, :], in_=PS[bank]),
                     reads=[("ps", bank)], writes=[("Vst", sl, t)])
                if blk > 0:
                    bank = pbank.next()
                    proj_tok(hT_b[sl], hres_l, t, W0a, (0, 512), W0a_res, bank)
                    S.op("act", lambda e, bank=bank, t=t: e.activation(out=Fst[sl][:, t, :], in_=PS[bank], func=AF.Copy),
                         reads=[("ps", bank)], writes=[("Fst", sl, t)])
                while deferred:
                    deferred.pop(0)()
                deferred.append(part_b)
            deferred.append(lambda sl=sl, key0=key0, n=n, nt=nt: dma(KT_scr[:, :, key0:key0 + n], KTst[sl][:, :, 0:n],
                                                                  reads=[(("KTst", sl), t) for t in range(nt)]))
            dma(V_scr[key0:key0 + n, :].rearrange("(t p) c -> p t c", p=128), Vst[sl][:, 0:nt, :],
                reads=[("Vst", sl, t) for t in range(nt)])
            if blk > 0:
                dma(F_scr[(blk - 1) * 512:blk * 512, :].rearrange("(t p) c -> p t c", p=128), Fst[sl],
                    reads=[("Fst", sl, t) for t in range(4)])
        while deferred:
            deferred.pop(0)()
        S.barrier(tiny)
        if stop_after == "P1a":
            S.emit()
            return nc

        own_blocks = [(i * 512, 4) for i in range(8)] + [(4096, 1)]
        wm.reset()
        W0b_res = [("W0bQ", k) for k in range(8)] + [("W0bG", k) for k in range(8)]
        xin_b = [wm.alloc([4, 1024], F32) for _ in range(2)]
        xs_b = [wm.alloc([4, 1024], BF16) for _ in range(2)]
        hT_b = [wm.alloc([8, 512], BF16) for _ in range(2)]
        junk = wm.alloc([2, 1024], BF16)
        ss_b = [wm.alloc([12], F32) for _ in range(2)]
        rope_b = [wm.alloc([4, 128], F32) for _ in range(3)]
        qtmp = [(wm.alloc([512], F32), wm.alloc([512], F32), wm.alloc([512], F32), wm.alloc([512], F32), wm.alloc([24], F32))
                for _ in range(2)]
        kb_b = [wm.alloc([512], BF16) for _ in range(5)]
        kbrot = Rot([0, 1, 2, 3, 4])
        QTst = [wm.alloc([4, 512], BF16) for _ in range(2)]
        GTst = [wm.alloc([8, 512], BF16) for _ in range(2)]
        def load_b(bi):
            tok0, nt = own_blocks[bi]
            sl = bi % 2
            n = nt * 128
            dma(xin_b[sl][:, 0:nt, :], x_own[tok0:tok0 + n, :].rearrange("(t p) d -> p t d", p=128), writes=[("xin", sl)])
            dma(rope_b[bi % 3][:, 0:nt, :], ropeq_d[tok0:tok0 + n, :].rearrange("(t p) c -> p t c", p=128), writes=[("rope", bi % 3)])

        def pre_b(bi):
            sl = bi % 2
            norm_pre(xin_b[sl], xin_res(sl), own_blocks[bi][1], xs_b[sl], ("xs", sl), junk, ss_b[sl])

        NB = len(own_blocks)
        load_b(0)
        load_b(1)
        pre_b(0)
        for bi, (tok0, nt) in enumerate(own_blocks):
            sl = bi % 2
            n = nt * 128
            if bi + 1 < NB:
                pre_b(bi + 1)
            if bi + 2 < NB:
                load_b(bi + 2)
            if bi == 0:
                norm_T(nt, 0, 1, hT_b[sl], ("hT", sl), xs_b[sl], ("xs", sl))
            hres_l = [(("hT", sl), fc) for fc in range(8)]
            defs = []
            for t in range(nt):
                bank = pbank.next()
                proj_tok(hT_b[sl], hres_l, t, W0b, (0, 512), W0b_res, bank)
                defs.append(qk_tile(bank, qg_bc, "qg", rope_b[bi % 3][:, t, :], ("rope", bi % 3), QTst[sl], ("QTst", sl), t))
            for gc in range(8):
                if gc == 5 and bi + 1 < NB:
                    norm_T(own_blocks[bi + 1][1], 0, 1, hT_b[1 - sl], ("hT", 1 - sl), xs_b[1 - sl], ("xs", 1 - sl))
                if gc >= 4 and defs:
                    defs.pop(0)()
                bank = pbank.next()
                for fc in range(8):
                    S.op("pe", lambda e, fc=fc, gc=gc, bank=bank: e.matmul(
                        PS[bank][:, 0:n], lhsT=W0b[:, fc, 512 + gc * 128:512 + (gc + 1) * 128], rhs=hT_b[sl][:, fc, 0:n],
                        start=(fc == 0), stop=(fc == 7)), reads=[hres_l[fc]] + W0b_res, writes=[("ps", bank)])
                S.op("act", lambda e, gc=gc, bank=bank: e.activation(out=GTst[sl][:, gc, 0:n], in_=PS[bank][:, 0:n], func=AF.Silu),
                     reads=[("ps", bank)], writes=[("GTst", sl, gc)])
            while defs:
                defs.pop(0)()
            dma(QT_scr[:, :, tok0:tok0 + n], QTst[sl][:, :, 0:n], reads=[(("QTst", sl), t) for t in range(nt)])
            dma(GT_scr[:, :, tok0:tok0 + n], GTst[sl][:, :, 0:n], reads=[("GTst", sl, gc) for gc in range(8)])
        S.barrier(tiny)
        if stop_after == "P1b":
            S.emit()
            return nc

        wm.limit = SBUF_BYTES
        wm.reset()
        Fs = wm.alloc([64, 512], BF16)
        T1 = wm.alloc([64, 2, 128], BF16)
        Zst = [wm.alloc([2, 512], BF16) for _ in range(3)]
        Fv = F_scr.rearrange("(p a) c -> p a c", a=64)
        for q4 in range(4):
            dma(Fs[:, q4 * 16:(q4 + 1) * 16, :], Fv[:, q4 * 16:(q4 + 1) * 16, :], writes=[("Fs", q4)])
            dma(T1[:, q4 * 16:(q4 + 1) * 16, :, :], t1_d[:, q4 * 16:(q4 + 1) * 16, :, :], writes=[("T1", q4)])
        zrot = Rot([0, 1, 2])
        prot = Rot(list(range(8)))
        for a in range(64):
            zi = zrot.next()
            for part in range(2):
                bank = prot.next()
                S.op("pe", lambda e, a=a, part=part, bank=bank: e.matmul(PS[bank], lhsT=T1[:, a, part, :], rhs=Fs[:, a, :], start=True, stop=True),
                     reads=[("Fs", a // 16), ("T1", a // 16)], writes=[("ps", bank)])
                if part == 0:
                    S.op("act", lambda e, bank=bank, zi=zi: e.activation(out=Zst[zi][:, 0, :], in_=PS[bank], func=AF.Copy),
                         reads=[("ps", bank)], writes=[("Zst", zi, 0)])
                else:
                    S.op("dve", lambda e, bank=bank, zi=zi: e.tensor_copy(out=Zst[zi][:, 1, :], in_=PS[bank]),
                         reads=[("ps", bank)], writes=[("Zst", zi, 1)])
            dma(Z_scr[:, a, :, :].rearrange("q k c -> k q c"), Zst[zi], reads=[("Zst", zi, 0), ("Zst", zi, 1)])
        S.barrier(tiny)
        wm.reset()
        Xs = wm.alloc([4, 2, 33 * 128], BF16)
        Zs = [wm.alloc([4, 512], BF16) for _ in range(3)]
        FOst = [wm.alloc([4, 512], BF16) for _ in range(2)]
        Zv = Z_scr.rearrange("q a k c -> (q a) k c")
        zrot = Rot([0, 1, 2])
        ei = 0
        for kc in range(32):
            zi = zrot.next()
            dma(Zs[zi], Zv[:, kc * 4:(kc + 1) * 4, :], writes=[("Zs", zi)])
            for g in range(4):
                bank = prot.next()
                for j in range(4):
                    S.op("pe", lambda e, g=g, j=j, bank=bank, zi=zi: e.matmul(PS[bank][:, j * 66:(j + 1) * 66], lhsT=Zs[zi][:, j, g * 128:(g + 1) * 128],
                                                                       rhs=t2, start=True, stop=True),
                         reads=[("Zs", zi), "t2"], writes=[("ps", bank)])
                src = PS[bank][:, 0:264].rearrange("p (j q a) -> p q a j", j=4, q=2)
                dst = Xs[:, g, :, :].rearrange("p q (a k) -> p q a k", k=128)[:, :, :, kc * 4:(kc + 1) * 4]
                eng = "act" if (ei % 2 == 0) else "dve"
                ei += 1
                for part in range(2):
                    if eng == "act":
                        S.op("act", lambda e, s=src[:, part], d=dst[:, part]: e.activation(out=d, in_=s, func=AF.Copy),
                             reads=[("ps", bank)], writes=[("Xs", g, kc, part)])
                    else:
                        S.op("dve", lambda e, s=src[:, part], d=dst[:, part]: e.tensor_copy(out=d, in_=s),
                             reads=[("ps", bank)], writes=[("Xs", g, kc, part)])
        xs_all = [("Xs", g, kc, part) for g in range(4) for kc in range(32) for part in range(2)]
        for bi, (tok0, nt) in enumerate(own_blocks):
            sl = bi % 2
            n = nt * 128
            for g in range(4):
                bank = prot.next()
                for part in range(2):
                    S.op("pe", lambda e, g=g, part=part, bank=bank: e.matmul(PS[bank][:, 0:n], lhsT=ccsc[:, part, :], rhs=Xs[:, g, part, tok0:tok0 + n],
                                                                      start=(part == 0), stop=(part == 1)),
                         reads=(xs_all if (bi == 0 and g == 0 and part == 0) else []) + ["ccsc"], writes=[("ps", bank)])
                if g % 2 == 0:
                    S.op("act", lambda e, g=g, bank=bank: e.activation(out=FOst[sl][:, g, 0:n], in_=PS[bank][:, 0:n], func=AF.Copy),
                         reads=[("ps", bank)], writes=[("FOst", sl, g)])
                else:
                    S.op("dve", lambda e, g=g, bank=bank: e.tensor_copy(out=FOst[sl][:, g, 0:n], in_=PS[bank][:, 0:n]),
                         reads=[("ps", bank)], writes=[("FOst", sl, g)])
            dma(FO_scr[:, :, tok0:tok0 + n], FOst[sl][:, :, 0:n], reads=[("FOst", sl, g) for g in range(4)])
        S.barrier(tiny)
        if stop_after == "P2":
            S.emit()
            return nc

        wm.reset()
        KT = wm.alloc([4, NKEY], BF16)
        Vs = wm.alloc([NKT, 512], BF16)
        Qb = [wm.alloc([4, 512], BF16) for _ in range(2)]
        NPT = 6
        Ptb = [wm.alloc([2, 512], BF16) for _ in range(NPT)]
        Pt = [[Ptb[i][:, m, :] for i in range(NPT)] for m in range(2)]
        accb = [wm.alloc([2, 512], F32) for _ in range(2)]
        acc = [[accb[p][:, m, :] for m in range(2)] for p in range(2)]
        P2b = [wm.alloc([2, 512], BF16) for _ in range(2)]
        P4b = [wm.alloc([2, 512], BF16) for _ in range(2)]
        osb = [wm.alloc([512], F32) for _ in range(2)]
        fr = [wm.alloc([512], F32) for _ in range(3)]
        fsq = wm.alloc([512], BF16)
        AOst = [wm.alloc([4, 512], BF16) for _ in range(2)]
        Vv = V_scr.rearrange("(t p) c -> p t c", p=128)
        dma(KT[:, 0, :], KT_scr[:, 0, :], writes=[("KT", 0)])
        for q6 in range(6):
            dma(Vs[:, q6 * 11:(q6 + 1) * 11, :], Vv[:, q6 * 11:(q6 + 1) * 11, :], writes=[("Vs", q6)])
        for h in range(1, 4):
            dma(KT[:, h, :], KT_scr[:, h, :], writes=[("KT", h)], q="pool")
        prt = Rot(list(range(NPT)))
        pending = []
        hcount = 0

        def make_finalize(h, p, n, bsl, tok0, last_head):
            OB = [4 + 2 * p, 5 + 2 * p]

            def s1():
                for m in range(2):
                    S.op("dve", lambda e: e.tensor_copy(out=osb[m][:, 0:n], in_=PS[OB[m]][:, 0:n]), reads=[("ps", OB[m])], writes=[("osb", m)])
                S.op("pe", lambda e: e.matmul(PS[OB[0]][:, 0:n], lhsT=ones32, rhs=acc[p][0][:, 0:n], start=True, stop=True),
                     reads=[("acc", p, 0), "ones32"], writes=[("ps", OB[0])])
                S.op("pe", lambda e: e.matmul(PS[OB[1]][:, 0:n], lhsT=ones32, rhs=acc[p][1][:, 0:n], start=True, stop=True),
                     reads=[("acc", p, 1), "ones32"], writes=[("ps", OB[1])])

            def s2():
                for m in range(2):
                    S.op("dve", lambda e: e.reciprocal(out=fr[m][:, 0:n], in_=PS[OB[m]][:, 0:n]), reads=[("ps", OB[m])], writes=[("fr", m)])
                    S.op("dve", lambda e: e.tensor_tensor(out=osb[m][:, 0:n], in0=osb[m][:, 0:n], in1=fr[m][:, 0:n], op=ALU.mult),
                         reads=[("osb", m), ("fr", m)], writes=[("osb", m)])
                S.op("dve", lambda e: e.scalar_tensor_tensor(out=osb[0][:, 0:n], in0=osb[1][:, 0:n], scalar=neglam, in1=osb[0][:, 0:n],
                                                             op0=ALU.mult, op1=ALU.add), reads=[("osb", 0), ("osb", 1), "neglam"], writes=[("osb", 0)])

            def s3():
                S.op("act", lambda e: e.activation(out=fsq[:, 0:n], in_=osb[0][:, 0:n], func=AF.Square), reads=[("osb", 0)], writes=["fsq"])

            def s4():
                S.op("pe", lambda e: e.matmul(PS[OB[0]][:, 0:n], lhsT=onesm, rhs=fsq[:, 0:n], start=True, stop=True),
                     reads=["fsq", "onesm"], writes=[("ps", OB[0])])

            def s5():
                S.op("act", lambda e: e.activation(out=fr[2][:, 0:n], in_=PS[OB[0]][:, 0:n], func=AF.Ln, bias=epsv), reads=[("ps", OB[0]), "epsv"],
                     writes=[("fr", 2)])
                S.op("act", lambda e: e.activation(out=fr[2][:, 0:n], in_=fr[2][:, 0:n], func=AF.Exp, scale=-0.5), reads=[("fr", 2)], writes=[("fr", 2)])

            def s6():
                S.op("dve", lambda e: e.scalar_tensor_tensor(out=AOst[bsl][:, h, 0:n], in0=osb[0][:, 0:n], scalar=subln8, in1=fr[2][:, 0:n],
                                                             op0=ALU.mult, op1=ALU.mult),
                     reads=[("osb", 0), ("fr", 2), "subln8"], writes=[("AOst", bsl, h)])
                if last_head:
                    dma(AO_scr[:, :, tok0:tok0 + n], AOst[bsl][:, :, 0:n], reads=[("AOst", bsl, hh) for hh in range(4)])
            return [(1, s1), (3, s2), (14, s3), (16, s4), (18, s5), (19, s6)]

        def attn_head(bi, tok0, nt, h, p):
            sl = bi % 2
            n = nt * 128
            OB = [4 + 2 * p, 5 + 2 * p]

            def qk(kt):
                s2_ = kt % 2
                for m in range(2):
                    bank = s2_ * 2 + m
                    S.op("pe", lambda e: e.matmul(PS[bank][:, 0:n], lhsT=KT[m * 64:(m + 1) * 64, h, kt * 128:(kt + 1) * 128],
                                                  rhs=Qb[sl][m * 64:(m + 1) * 64, h, 0:n], start=True, stop=True),
                         reads=[("KT", h), ("Qb", sl)], writes=[("ps", bank)])

            def pv(kt, pi):
                for m in range(2):
                    S.op("pe", lambda e: e.matmul(PS[OB[m]][:, 0:n], lhsT=Vs[:, kt, h * 128:(h + 1) * 128], rhs=Pt[m][pi][:, 0:n],
                                                  start=(kt == 0), stop=(kt == NKT - 1)),
                         reads=[("Pt", m, pi), ("Vs", kt // 11)], writes=[("ps", OB[m])])

            qk(0)
            pi_prev = None
            for kt in range(NKT):
                s2_ = kt % 2
                pi = prt.next()
                S.op("act", lambda e: e.activation(out=Ptb[pi][:, :, 0:n],
                                                   in_=PSALL[:, s2_ * 1024:(s2_ + 1) * 1024].rearrange("p (m q) -> p m q", m=2)[:, :, 0:n],
                                                   func=AF.Exp, scale=0.125),
                     reads=[("ps", s2_ * 2), ("ps", s2_ * 2 + 1)], writes=[("Pt", 0, pi), ("Pt", 1, pi)])
                if kt + 1 < NKT:
                    qk(kt + 1)
                if kt >= 1:
                    pv(kt - 1, pi_prev)
                if kt % 2 == 1:
                    r2 = (kt // 2) % 2
                    S.op("dve", lambda e: e.tensor_tensor(out=P2b[r2][:, :, 0:n], in0=Ptb[pi_prev][:, :, 0:n], in1=Ptb[pi][:, :, 0:n], op=ALU.add),
                         reads=[("Pt", 0, pi_prev), ("Pt", 1, pi_prev), ("Pt", 0, pi), ("Pt", 1, pi)], writes=[("P2", r2)])
                    src = None
                    if kt % 4 == 3:
                        r4 = (kt // 4) % 2
                        S.op("dve", lambda e: e.tensor_tensor(out=P4b[r4][:, :, 0:n], in0=P2b[0][:, :, 0:n], in1=P2b[1][:, :, 0:n], op=ALU.add),
                             reads=[("P2", 0), ("P2", 1)], writes=[("P4", r4)])
                        src, sres = P4b[r4], ("P4", r4)
                    elif kt == NKT - 1:
                        src, sres = P2b[r2], ("P2", r2)
                    if src is not None:
                        if kt == 3:
                            S.op("dve", lambda e: e.tensor_copy(out=accb[p][:, :, 0:n], in_=src[:, :, 0:n]),
                                 reads=[sres], writes=[("acc", p, 0), ("acc", p, 1)])
                        else:
                            S.op("dve", lambda e: e.tensor_tensor(out=accb[p][:, :, 0:n], in0=accb[p][:, :, 0:n], in1=src[:, :, 0:n], op=ALU.add),
                                 reads=[sres, ("acc", p, 0), ("acc", p, 1)], writes=[("acc", p, 0), ("acc", p, 1)])
                pi_prev = pi
                while pending and pending[0][0] <= kt:
                    pending.pop(0)[1]()
            pv(NKT - 1, pi_prev)

        dma(Qb[0][:, :, 0:512], QT_scr[:, :, 0:512], writes=[("Qb", 0)])
        for bi, (tok0, nt) in enumerate(own_blocks):
            if bi + 1 < len(own_blocks):
                t0n, ntn = own_blocks[bi + 1]
                dma(Qb[1 - bi % 2][:, :, 0:ntn * 128], QT_scr[:, :, t0n:t0n + ntn * 128], writes=[("Qb", 1 - bi % 2)])
            for h in range(4):
                p = hcount % 2
                hcount += 1
                attn_head(bi, tok0, nt, h, p)
                assert not pending
                pending.extend(make_finalize(h, p, nt * 128, bi % 2, tok0, h == 3))
        while pending:
            pending.pop(0)[1]()
        S.barrier(tiny)
        if stop_after == "P3":
            S.emit()
            return nc

        wm.reset()
        Wo0 = wm.alloc([8, 1024], BF16)
        W1a = wm.alloc([8, 2048], BF16)
        Wo0_res = load_w(Wo0, w_out0, "Wo0")
        W1a_res = load_w(W1a, w_in1[:, 1024:3072], "W1a")
        FAb = wm.alloc([8, 512], BF16)
        GTb = wm.alloc([8, 512], BF16)
        yT = wm.alloc([8, 512], BF16)
        xin_b = [wm.alloc([4, 1024], F32) for _ in range(2)]
        x1b = [wm.alloc([4, 1024], F32) for _ in range(2)]
        tmpb = [wm.alloc([512], F32) for _ in range(2)]
        xs1 = wm.alloc([4, 1024], BF16)
        h1T = wm.alloc([8, 512], BF16)
        junk = wm.alloc([2, 1024], BF16)
        ss1 = wm.alloc([12], F32)
        cgs = [wm.alloc([512], F32) for _ in range(2)]
        ust = [wm.alloc([8, 512], BF16) for _ in range(2)]
        zcol = wm.alloc([8, 1], BF16)
        S.op("pool", lambda e: e.memset(zcol, 0.0), writes=["zcol"])
        trot = Rot([0, 1])
        crot = Rot([0, 1])

        def load_fg(bi):
            tok0, nt = own_blocks[bi]
            n = nt * 128
            dma(FAb[:, 0:4, 0:n], FO_scr[:, :, tok0:tok0 + n], writes=[("FAb", 0)])
            dma(FAb[:, 4:8, 0:n], AO_scr[:, :, tok0:tok0 + n], writes=[("FAb", 1)])
            dma(GTb[:, :, 0:n], GT_scr[:, :, tok0:tok0 + n], writes=["GTb"])

        def load_x(bi):
            tok0, nt = own_blocks[bi]
            dma(xin_b[bi % 2][:, 0:nt, :], x_own[tok0:tok0 + nt * 128, :].rearrange("(t p) d -> p t d", p=128), writes=[("xin", bi % 2)])

        def outproj0(bi):
            tok0, nt = own_blocks[bi]
            sl = bi % 2
            n = nt * 128
            S.op("dve", lambda e: e.tensor_tensor(out=yT[:, 0:4, 0:n], in0=FAb[:, 0:4, 0:n], in1=GTb[:, 0:4, 0:n], op=ALU.mult),
                 reads=[("FAb", 0), "GTb"], writes=[("yT", 0)])
            S.op("pool", lambda e: e.tensor_tensor(out=yT[:, 4:8, 0:n], in0=FAb[:, 4:8, 0:n], in1=GTb[:, 4:8, 0:n], op=ALU.mult),
                 reads=[("FAb", 1), "GTb"], writes=[("yT", 1)])
            if bi + 1 < NB:
                load_fg(bi + 1)
            for t in range(nt):
                for hf in range(2):
                    bank = pbank.next()
                    for fc in range(8):
                        S.op("pe", lambda e: e.matmul(PS[bank], lhsT=yT[:, fc, t * 128:(t + 1) * 128], rhs=Wo0[:, fc, hf * 512:(hf + 1) * 512],
                                                      start=(fc == 0), stop=(fc == 7)), reads=[("yT", fc // 4)] + Wo0_res, writes=[("ps", bank)])
                    ti = trot.next()
                    S.op("dve", lambda e: e.tensor_tensor(out=tmpb[ti], in0=PS[bank], in1=gate_bc[0][:, hf * 512:(hf + 1) * 512], op=ALU.mult),
                         reads=[("ps", bank)], writes=[("tmpb", ti)])
                    S.op("pool", lambda e: e.tensor_tensor(out=x1b[sl][:, t, hf * 512:(hf + 1) * 512], in0=tmpb[ti],
                                                           in1=xin_b[sl][:, t, hf * 512:(hf + 1) * 512], op=ALU.add),
                         reads=[("tmpb", ti), ("xin", sl)], writes=[("x1b", sl, t, hf)])
            dma(X1_scr[tok0:tok0 + n, :].rearrange("(t p) d -> p t d", p=128), x1b[sl][:, 0:nt, :],
                reads=[("x1b", sl, t, hf) for t in range(nt) for hf in range(2)])

        load_fg(0)
        load_x(0)
        load_x(1)
        outproj0(0)
        for bi, (tok0, nt) in enumerate(own_blocks):
            sl = bi % 2
            n = nt * 128
            norm_pre(x1b[sl], (lambda t, sl=sl: [("x1b", sl, t, 0), ("x1b", sl, t, 1)]), nt, xs1, "xs1", junk, ss1)
            if bi + 1 < NB:
                outproj0(bi + 1)
            if bi + 2 < NB:
                load_x(bi + 2)
            norm_T(nt, 4, 5, h1T, "h1T", xs1, "xs1")
            h1res = [("h1T", fc) for fc in range(8)]
            for cc in range(8):
                b1 = pbank.next()
                b2 = pbank.next()
                for which, bank in ((0, b1), (1, b2)):
                    for fc in range(8):
                        S.op("pe", lambda e: e.matmul(PS[bank][:, 0:n], lhsT=W1a[:, fc, which * 1024 + cc * 128:which * 1024 + (cc + 1) * 128],
                                                      rhs=h1T[:, fc, 0:n], start=(fc == 0), stop=(fc == 7)),
                             reads=[h1res[fc]] + W1a_res, writes=[("ps", bank)])
                ci = crot.next()
                S.op("act", lambda e: e.activation(out=cgs[ci][:, 0:n], in_=PS[b1][:, 0:n], func=AF.Copy),
                     reads=[("ps", b1)], writes=[("cgs", ci)])
                S.op("dve", lambda e: e.tensor_tensor(out=ust[sl][:, cc, 0:n], in0=cgs[ci][:, 0:n], in1=PS[b2][:, 0:n], op=ALU.mult),
                     reads=[("ps", b2), ("cgs", ci)], writes=[("ust", sl, cc)])
            dma(UT_scr[:, :, 1 + tok0:1 + tok0 + n], ust[sl][:, :, 0:n], reads=[("ust", sl, cc) for cc in range(8)])
        dma(UT_scr[:, :, 0:1], zcol, reads=["zcol"], allow_slow_non_contiguous=True)
        dma(UT_scr[:, :, NOWN + 1:NOWN + 2], zcol, reads=["zcol"], allow_slow_non_contiguous=True)
        S.barrier(tiny)
        if stop_after == "P4a":
            S.emit()
            return nc

        wm.reset()
        W1b = wm.alloc([8, 2048], BF16)
        Wo1 = wm.alloc([8, 1024], BF16)
        W1b_res = load_w(W1b[:, :, 0:1024], w_in1[:, 0:1024], "W1bB") + load_w(W1b[:, :, 1024:2048], w_in1[:, 3072:4096], "W1bG")
        Wo1_res = load_w(Wo1, w_out1, "Wo1")
        ub = [wm.alloc([8, 514], BF16) for _ in range(2)]
        xin_b = [wm.alloc([4, 1024], F32) for _ in range(3)]
        x2b = wm.alloc([4, 1024], F32)
        tmpb = [wm.alloc([512], F32) for _ in range(2)]
        xs1 = [wm.alloc([4, 1024], BF16) for _ in range(2)]
        h1T = wm.alloc([8, 512], BF16)
        junk = wm.alloc([2, 1024], BF16)
        ss1 = [wm.alloc([12], F32) for _ in range(2)]
        sgb = [wm.alloc([512], F32) for _ in range(2)]
        vb = [wm.alloc([512], F32) for _ in range(2)]
        cvb = [wm.alloc([512], F32) for _ in range(2)]
        y1T = wm.alloc([8, 512], BF16)

        def load_x1(bi):
            tok0, nt = own_blocks[bi]
            dma(xin_b[bi % 3][:, 0:nt, :], X1_scr[tok0:tok0 + nt * 128, :].rearrange("(t p) d -> p t d", p=128), writes=[("xin", bi % 3)])

        def load_u(bi):
            tok0, nt = own_blocks[bi]
            dma(ub[bi % 2][:, :, 0:nt * 128 + 2], UT_scr[:, :, tok0:tok0 + nt * 128 + 2], writes=[("ub", bi % 2)])

        def pre_c(bi):
            norm_pre(xin_b[bi % 3], (lambda t, k=bi % 3: [("xin", k)]), own_blocks[bi][1], xs1[bi % 2], ("xs1", bi % 2), junk, ss1[bi % 2])

        load_x1(0)
        load_x1(1)
        load_u(0)
        pre_c(0)
        for bi, (tok0, nt) in enumerate(own_blocks):
            sl = bi % 2
            x3 = bi % 3
            n = nt * 128
            if bi + 1 < NB:
                pre_c(bi + 1)
            if bi + 2 < NB:
                load_x1(bi + 2)
            if bi + 1 < NB:
                load_u(bi + 1)
            norm_T(nt, 4, 5, h1T, "h1T", xs1[sl], ("xs1", sl))
            h1res = [("h1T", fc) for fc in range(8)]
            for cc in range(8):
                b1 = pbank.next()
                b2 = pbank.next()
                for which, bank in ((0, b1), (1, b2)):
                    for fc in range(8):
                        S.op("pe", lambda e: e.matmul(PS[bank][:, 0:n], lhsT=W1b[:, fc, which * 1024 + cc * 128:which * 1024 + (cc + 1) * 128],
                                                      rhs=h1T[:, fc, 0:n], start=(fc == 0), stop=(fc == 7)),
                             reads=[h1res[fc]] + W1b_res, writes=[("ps", bank)])
                ci = crot.next()
                S.op("act", lambda e: e.activation(out=sgb[ci][:, 0:n], in_=PS[b2][:, 0:n], func=AF.Silu),
                     reads=[("ps", b2)], writes=[("sgb", ci)])
                S.op("dve", lambda e: e.tensor_tensor(out=vb[ci][:, 0:n], in0=PS[b1][:, 0:n], in1=sgb[ci][:, 0:n], op=ALU.mult),
                     reads=[("ps", b1), ("sgb", ci)], writes=[("vb", ci)])
                S.op("dve", lambda e: e.tensor_scalar(out=cvb[ci][:, 0:n], in0=ub[sl][:, cc, 1:n + 1], scalar1=conv_sb[:, cc, 1:2],
                                                      scalar2=None, op0=ALU.mult), reads=[("ub", sl), "conv"], writes=[("cvb", ci)])
                S.op("dve", lambda e: e.scalar_tensor_tensor(out=cvb[ci][:, 0:n], in0=ub[sl][:, cc, 0:n], scalar=conv_sb[:, cc, 0:1],
                                                             in1=cvb[ci][:, 0:n], op0=ALU.mult, op1=ALU.add),
                     reads=[("ub", sl), "conv", ("cvb", ci)], writes=[("cvb", ci)])
                S.op("dve", lambda e: e.scalar_tensor_tensor(out=cvb[ci][:, 0:n], in0=ub[sl][:, cc, 2:n + 2], scalar=conv_sb[:, cc, 2:3],
                                                             in1=cvb[ci][:, 0:n], op0=ALU.mult, op1=ALU.add),
                     reads=[("ub", sl), "conv", ("cvb", ci)], writes=[("cvb", ci)])
                S.op("pool", lambda e: e.tensor_tensor(out=y1T[:, cc, 0:n], in0=cvb[ci][:, 0:n], in1=vb[ci][:, 0:n], op=ALU.mult),
                     reads=[("cvb", ci), ("vb", ci)], writes=[("y1T", cc)])
            for t in range(nt):
                for hf in range(2):
                    bank = pbank.next()
                    for fc in range(8):
                        S.op("pe", lambda e: e.matmul(PS[bank], lhsT=y1T[:, fc, t * 128:(t + 1) * 128], rhs=Wo1[:, fc, hf * 512:(hf + 1) * 512],
                                                      start=(fc == 0), stop=(fc == 7)), reads=[("y1T", fc)] + Wo1_res, writes=[("ps", bank)])
                    ti = trot.next()
                    S.op("dve", lambda e: e.tensor_tensor(out=tmpb[ti], in0=PS[bank], in1=gate_bc[1][:, hf * 512:(hf + 1) * 512], op=ALU.mult),
                         reads=[("ps", bank)], writes=[("tmpb", ti)])
                    S.op("pool", lambda e: e.tensor_tensor(out=x2b[:, t, hf * 512:(hf + 1) * 512], in0=tmpb[ti],
                                                           in1=xin_b[x3][:, t, hf * 512:(hf + 1) * 512], op=ALU.add),
                         reads=[("tmpb", ti), ("xin", x3)], writes=[("x2b", t, hf)])
            dma(out_d[tok0:tok0 + n, :].rearrange("(t p) d -> p t d", p=128), x2b[:, 0:nt, :],
                reads=[("x2b", t, hf) for t in range(nt) for hf in range(2)])
        S.emit()
    return nc


def _rope_tables(pos_r, pos_c):
    half = 32
    freqs = 10000.0 ** (-np.arange(0, half, 2, dtype=np.float64) / half)
    ar = pos_r[:, None] * freqs
    ac = pos_c[:, None] * freqs
    n = pos_r.shape[0]
    TC = np.concatenate([np.cos(ar), np.cos(ar), np.cos(ac), np.cos(ac)], axis=1)
    TS = np.concatenate([-np.sin(ar), np.sin(ar), -np.sin(ac), np.sin(ac)], axis=1)
    return np.concatenate([TC, TS], axis=1).astype(np.float32)


def _consts():
    bf = ml_dtypes.bfloat16
    pos = np.arange(NTOK)
    rope_lat = _rope_tables((pos // 64).astype(np.float64), (pos % 64).astype(np.float64))
    rope_ctx = np.zeros((NCTX, 128), np.float32)
    rope_ctx[:, 0:64] = 1.0
    ropek = np.concatenate([rope_ctx, rope_lat], axis=0)
    p = np.arange(128, dtype=np.float64)[:, None, None]
    a = np.arange(64, dtype=np.float64)[None, :, None]
    kp = np.arange(128, dtype=np.float64)[None, None, :]
    th = 2 * np.pi * (p * kp / 128.0 + a * kp / 8192.0)
    t1 = np.stack([np.cos(th), -np.sin(th)], axis=2).astype(bf)
    c = np.arange(128, dtype=np.float64)
    thc = 2 * np.pi * np.outer(c, c) / 128.0
    ccsc = np.stack([np.cos(thc) / 1024.0, np.sin(thc) / 1024.0], axis=1).astype(bf)
    t2s = []
    for s in range(2):
        ka = (31 * s + np.arange(33)).astype(np.float64)
        aa = np.arange(64, dtype=np.float64)
        th2 = 2 * np.pi * np.outer(aa, ka) / 64.0
        top = np.concatenate([np.cos(th2), -np.sin(th2)], axis=1)
        bot = np.concatenate([np.sin(th2), np.cos(th2)], axis=1)
        t2s.append(np.concatenate([top, bot], axis=0).astype(bf))
    ident = np.eye(128, dtype=np.float32).astype(bf)
    return rope_lat, ropek, t1, ccsc, t2s, ident


def make_in_maps(x, c, ctx, c_ctx, norm_g, ada_w, ada_b, even_w_in, even_q_norm, even_k_norm,
                 even_lambda_q1, even_lambda_k1, even_lambda_q2, even_lambda_k2, even_subln,
                 even_w_out, odd_w_in, odd_conv_w, odd_w_out):
    f = lambda a: np.ascontiguousarray(np.asarray(a, dtype=np.float32))
    x, c, ctx, c_ctx, norm_g, ada_w, ada_b = map(f, (x, c, ctx, c_ctx, norm_g, ada_w, ada_b))
    rope_lat, ropek, t1, ccsc, t2s, ident = _consts()
    w_in0 = f(even_w_in)[0]
    w_out0 = f(even_w_out)[0]
    w_in1 = f(odd_w_in)[0]
    w_out1 = f(odd_w_out)[0]
    adab_l = f(ada_b).reshape(2, 24, 128).transpose(2, 0, 1)
    normg_l = f(norm_g).reshape(2, 8, 128).transpose(2, 0, 1)
    qkg = np.stack([f(even_q_norm)[0], f(even_k_norm)[0]], axis=0)
    lam = np.concatenate([f(even_lambda_q1)[0], f(even_lambda_k1)[0], f(even_lambda_q2)[0], f(even_lambda_k2)[0]])
    subln = f(even_subln)[0].reshape(128, 1)
    conv = f(odd_conv_w)[0].reshape(3, 8, 128).transpose(2, 1, 0)
    cctx_l = c_ctx.reshape(8, 128).T
    maps = []
    for core in range(8):
        b, s = core // 2, core % 2
        q0 = 3968 * s
        cc = np.stack([c[b].reshape(8, 128).T, cctx_l], axis=2)
        maps.append({
            "x_all": x[b], "x_own": np.ascontiguousarray(x[b, q0:q0 + NOWN]), "ctx": ctx[b],
            "cc": np.ascontiguousarray(cc), "ada_w": ada_w, "ada_b": np.ascontiguousarray(adab_l),
            "norm_g": np.ascontiguousarray(normg_l), "w_in0": w_in0, "w_out0": w_out0, "w_in1": w_in1, "w_out1": w_out1,
            "qk_g": np.ascontiguousarray(qkg), "lam": np.ascontiguousarray(lam), "subln": np.ascontiguousarray(subln),
            "conv_w": np.ascontiguousarray(conv), "ropek": ropek, "ropeq": np.ascontiguousarray(rope_lat[q0:q0 + NOWN]),
            "t1": t1, "t2": t2s[s], "ccsc": ccsc, "ident": ident,
        })
    return maps


_NC_CACHE = {}


def kernel(**inputs):
    maps = make_in_maps(**inputs)
    if "nc" not in _NC_CACHE:
        _NC_CACHE["nc"] = build()
    res = run_bass_kernel_spmd(_NC_CACHE["nc"], maps, core_ids=list(range(8)))
    out = np.empty((4, NTOK, D), np.float32)
    for core in range(8):
        b, s = core // 2, core % 2
        o = res.results[core]["out"]
        if s == 0:
            out[b, 0:4096] = o[0:4096]
        else:
            out[b, 4096:8192] = o[128:NOWN]
    return out
```

```python
import contextlib
import numpy as np
import ml_dtypes
import concourse.bass as bass
import concourse.mybir as mybir
from concourse.bass_utils import run_bass_kernel_spmd

F32 = mybir.dt.float32
BF16 = mybir.dt.bfloat16
ALU = mybir.AluOpType
AF = mybir.ActivationFunctionType
AX = mybir.AxisListType

NTOK = 8192
D = 1024
NCTX = 256
NOWN = 4224
NKEY = NTOK + NCTX
NKT = NKEY // 128
EPS = 1e-6
LAM_INIT = 0.2

ENGS = ("pe", "act", "dve", "pool", "sp")
NDMA_SEM = 8
SEM_CHUNK = 24000


class Instr:
    __slots__ = ("eng", "fn", "deps", "is_dma", "idx", "dma_tok", "signaled", "sigidx")

    def __init__(self, eng, fn, is_dma):
        self.eng = eng
        self.fn = fn
        self.is_dma = is_dma
        self.deps = {}
        self.dma_tok = None
        self.signaled = False
        self.sigidx = -1


class _Rec:
    def __init__(self):
        self.call = None

    def __getattr__(self, name):
        def f(*args, **kw):
            self.call = (name, args, kw)
            return self
        return f


class Sched:
    def __init__(self, nc):
        self.nc = nc
        self.lists = {e: [] for e in ENGS}
        self.res = {}
        self.dma_count = {e: 0 for e in ENGS}

    def _add_dep(self, ins, tok):
        if tok is None:
            return
        key, val = tok
        if key == ins.eng and ins.eng == "pe" and not ins.is_dma:
            return
        if ins.deps.get(key, -1) < val:
            ins.deps[key] = val

    def op(self, eng, fn, reads=(), writes=(), dma=False):
        rec = _Rec()
        fn(rec)
        name, args, kw = rec.call
        ins = Instr(eng, (lambda e, name=name, args=args, kw=kw: getattr(e, name)(*args, **kw)), dma)
        lst = self.lists[eng]
        ins.idx = len(lst)
        writes = list(writes) + [r for r in reads if isinstance(r, tuple) and r[0] == "ps"]
        reads = [r for r in reads if not (isinstance(r, tuple) and r[0] == "ps")] + ["PHASE"]
        if dma:
            j = self.dma_count[eng]
            self.dma_count[eng] = j + 1
            slot = j % NDMA_SEM
            use = j // NDMA_SEM
            key = ("dma", eng, slot)
            if use > 0:
                ins.deps[key] = 16 * use
            tok = (key, 16 * (use + 1))
            ins.dma_tok = tok
        else:
            tok = (eng, ins.idx)
        for r in reads:
            ent = self.res.get(r)
            if ent is not None:
                self._add_dep(ins, ent[0])
        for w in writes:
            ent = self.res.get(w)
            if ent is not None:
                self._add_dep(ins, ent[0])
                for k, v in ent[1].items():
                    self._add_dep(ins, (k, v))
        for r in reads:
            ent = self.res.setdefault(r, [None, {}])
            if ent[1].get(tok[0], -1) < tok[1]:
                ent[1][tok[0]] = tok[1]
        for w in writes:
            self.res[w] = [tok, {}]
        lst.append(ins)
        return ins

    def barrier(self, tiny):
        ins = Instr("pool", lambda e: e.memset(tiny, 0.0), False)
        lst = self.lists["pool"]
        ins.idx = len(lst)
        ent = self.res.get("PHASE")
        if ent is not None:
            self._add_dep(ins, ent[0])
            for k, v in ent[1].items():
                self._add_dep(ins, (k, v))
        self.res["PHASE"] = [("pool", ins.idx), {}]
        lst.append(ins)

    def emit(self):
        nc = self.nc
        for e in ENGS:
            for ins in self.lists[e]:
                for key, val in ins.deps.items():
                    if isinstance(key, str):
                        self.lists[key][val].signaled = True
        nsig = {}
        for e in ENGS:
            c = 0
            for ins in self.lists[e]:
                if not ins.is_dma and ins.signaled:
                    ins.sigidx = c
                    c += 1
            nsig[e] = c
        with contextlib.ExitStack() as st:
            csems = {}
            for e in ENGS:
                n = (nsig[e] + SEM_CHUNK - 1) // SEM_CHUNK
                csems[e] = [st.enter_context(nc.semaphore(f"c_{e}_{i}")) for i in range(max(n, 1))]
            dsems = {}
            for e in ENGS:
                if self.dma_count[e] > 0:
                    for s in range(NDMA_SEM):
                        dsems[("dma", e, s)] = st.enter_context(nc.semaphore(f"d_{e}_{s}"))
            block = st.enter_context(nc.Block())
            lists = self.lists
            dma_count = self.dma_count

            def run(engname, eng):
                waited = {}
                for ins in lists[engname]:
                    for key, val in ins.deps.items():
                        if isinstance(key, str):
                            sidx = lists[key][val].sigidx
                            ch = sidx // SEM_CHUNK
                            v = sidx % SEM_CHUNK + 1
                            if (ch, v) <= waited.get(key, (-1, -1)):
                                continue
                            waited[key] = (ch, v)
                            eng.wait_ge(csems[key][ch], v)
                        else:
                            if waited.get(key, -1) >= val:
                                continue
                            waited[key] = val
                            eng.wait_ge(dsems[key], val)
                    r = ins.fn(eng)
                    if ins.is_dma:
                        r.then_inc(dsems[ins.dma_tok[0]], 16)
                    elif ins.signaled:
                        r.then_inc(csems[engname][ins.sigidx // SEM_CHUNK], 1)
                if engname == "sp":
                    for e in ENGS:
                        n = dma_count[e]
                        for s in range(NDMA_SEM):
                            uses = (n - s + NDMA_SEM - 1) // NDMA_SEM if n > s else 0
                            if uses > 0:
                                eng.wait_ge(dsems[("dma", e, s)], 16 * uses)

            @block.tensor
            def _(eng):
                run("pe", eng)

            @block.scalar
            def _(eng):
                run("act", eng)

            @block.vector
            def _(eng):
                run("dve", eng)

            @block.gpsimd
            def _(eng):
                run("pool", eng)

            @block.sync
            def _(eng):
                run("sp", eng)


class Mem:
    def __init__(self, big, base, limit):
        self.big = big
        self.base = base
        self.off = base
        self.limit = limit

    def reset(self):
        self.off = self.base

    def alloc(self, shape, dt):
        esz = 4 if dt == F32 else 2
        n = int(np.prod(shape)) * esz
        off = (self.off + 63) // 64 * 64
        assert off + n <= self.limit, (off, n, self.limit)
        self.off = off + n
        ap = self.big[:, off // 2:(off + n) // 2]
        if dt == F32:
            ap = ap.bitcast(F32)
        if len(shape) == 2:
            ap = ap.rearrange("p (a b) -> p a b", a=shape[0])
        elif len(shape) == 3:
            ap = ap.rearrange("p (a b c) -> p a b c", a=shape[0], b=shape[1])
        elif len(shape) == 4:
            ap = ap.rearrange("p (a b c d) -> p a b c d", a=shape[0], b=shape[1], c=shape[2])
        return ap


class Rot:
    def __init__(self, items):
        self.items = items
        self.i = 0

    def next(self):
        r = self.items[self.i % len(self.items)]
        self.i += 1
        return r


SBUF_BYTES = 203 * 1024
PERSIST = 16 * 1024


def build(stop_after=None, debug=False):
    nc = bass.Bass("TRN2", target_bir_lowering=False)

    def din(name, shape, dt=F32):
        return nc.dram_tensor(name, list(shape), dt, kind="ExternalInput").ap()

    def dscr(name, shape, dt):
        return nc.dram_tensor(name, list(shape), dt, kind="ExternalOutput" if debug else "Internal").ap()

    x_all = din("x_all", [NTOK, D])
    x_own = din("x_own", [NOWN, D])
    ctx_d = din("ctx", [NCTX, D])
    cc_d = din("cc", [128, 8, 2])
    ada_w = din("ada_w", [2, D, 3 * D])
    ada_b = din("ada_b", [128, 2, 24])
    normg_d = din("norm_g", [128, 2, 8])
    w_in0 = din("w_in0", [D, 3072])
    w_out0 = din("w_out0", [D, D])
    w_in1 = din("w_in1", [D, 4096])
    w_out1 = din("w_out1", [D, D])
    qkg_d = din("qk_g", [2, 64])
    lam_d = din("lam", [256])
    subln_d = din("subln", [128, 1])
    conv_d = din("conv_w", [128, 8, 3])
    ropek_d = din("ropek", [NKEY, 128])
    ropeq_d = din("ropeq", [NOWN, 128])
    t1_d = din("t1", [128, 64, 2, 128], BF16)
    t2_d = din("t2", [128, 66], BF16)
    ccsc_d = din("ccsc", [128, 2, 128], BF16)
    ident_d = din("ident", [128, 128], BF16)
    out_d = nc.dram_tensor("out", [NOWN, D], F32, kind="ExternalOutput").ap()

    KT_scr = dscr("KT_scr", [128, 4, NKEY], BF16)
    V_scr = dscr("V_scr", [NKEY, 512], BF16)
    F_scr = dscr("F_scr", [NTOK, 512], BF16)
    QT_scr = dscr("QT_scr", [128, 4, NOWN], BF16)
    GT_scr = dscr("GT_scr", [128, 8, NOWN], BF16)
    Z_scr = dscr("Z_scr", [2, 64, 128, 512], BF16)
    FO_scr = dscr("FO_scr", [128, 4, NOWN], BF16)
    AO_scr = dscr("AO_scr", [128, 4, NOWN], BF16)
    X1_scr = dscr("X1_scr", [NOWN, D], F32)
    UT_scr = dscr("UT_scr", [128, 8, NOWN + 2], BF16)
    dbg_d = nc.dram_tensor("dbg", [128, 256], F32, kind="ExternalOutput").ap() if debug else None

    with contextlib.ExitStack() as st:
        big = st.enter_context(nc.sbuf_tensor("big", [128, SBUF_BYTES // 2], BF16))
        PSALL = st.enter_context(nc.psum_tensor("psall", [128, 8 * 512], F32))[:]
        PS = [PSALL[:, i * 512:(i + 1) * 512] for i in range(8)]
        PSB = [p.bitcast(BF16) for p in PS]
        S = Sched(nc)
        pm = Mem(big, 0, PERSIST)
        wm = Mem(big, PERSIST, SBUF_BYTES)

        def dma(out, in_, reads=(), writes=(), q="sp", **kw):
            S.op(q, lambda e: e.dma_start(out=out, in_=in_, **kw), reads=reads, writes=writes, dma=True)

        def cast_dma(out, in_, reads=(), writes=()):
            S.op("pool", lambda e: e.dma_start(out=out, in_=in_, max_dma_last_dim=4096), reads=reads, writes=writes, dma=True)

        ident = pm.alloc([128], BF16)
        ones_bf = pm.alloc([128], BF16)
        onesm = pm.alloc([128], BF16)
        ident32 = pm.alloc([128], F32)
        ones32 = pm.alloc([128], F32)
        prm = pm.alloc([6, 8], F32)
        mod = [pm.alloc([24, 2], F32) for _ in range(2)]
        gate_bc = [pm.alloc([1024], F32) for _ in range(2)]
        qg_bc = pm.alloc([64], F32)
        kg_bc = pm.alloc([64], F32)
        neglam = pm.alloc([1], F32)
        subln8 = pm.alloc([1], F32)
        nhalf = pm.alloc([1], F32)
        epsv = pm.alloc([1], F32)
        tiny = pm.alloc([1], F32)
        conv_sb = pm.alloc([8, 3], F32)
        ccsc = pm.alloc([2, 128], BF16)
        t2 = pm.alloc([66], BF16)
        normg = pm.alloc([2, 8], F32)
        adab = pm.alloc([2, 24], F32)

        dma(ident, ident_d, writes=["ident"])
        dma(ccsc, ccsc_d, writes=["ccsc"])
        dma(t2, t2_d, writes=["t2"])
        dma(normg, normg_d, writes=["normg"])
        dma(adab, ada_b, writes=["adab"])
        dma(conv_sb, conv_d, writes=["conv"])
        dma(subln8, subln_d, writes=["subln8"])
        dma(qg_bc, qkg_d[0].partition_broadcast(128), writes=["qg"])
        dma(kg_bc, qkg_d[1].partition_broadcast(128), writes=["kg"])
        S.op("pool", lambda e: e.memset(ones_bf, 1.0), writes=["ones_bf"])
        S.op("pool", lambda e: e.memset(onesm, 1.0 / 128), writes=["onesm"])
        S.op("pool", lambda e: e.memset(ones32, 1.0), writes=["ones32"])
        S.op("pool", lambda e: e.memset(nhalf, -0.5), writes=["nhalf"])
        S.op("pool", lambda e: e.memset(epsv, EPS), writes=["epsv"])
        S.op("dve", lambda e: e.tensor_copy(out=ident32, in_=ident), reads=["ident"], writes=["ident32"])
        S.op("dve", lambda e: e.tensor_scalar(out=subln8, in0=subln8, scalar1=1.0 - LAM_INIT, scalar2=None, op0=ALU.mult),
             reads=["subln8"], writes=["subln8"])

        WTOP = 48 * 1024
        tm = Mem(big, SBUF_BYTES - WTOP, SBUF_BYTES)
        wm.limit = SBUF_BYTES - WTOP
        W0a = tm.alloc([8, 1536], BF16)
        W0b = tm.alloc([8, 1536], BF16)

        def emit_w0_loads():
            for k in range(8):
                cast_dma(W0a[:, k, 0:512], w_in0[k * 128:(k + 1) * 128, 0:512], writes=[("W0aF", k)])
                cast_dma(W0a[:, k, 512:1536], w_in0[k * 128:(k + 1) * 128, 1024:2048], writes=[("W0aKV", k)])
            for k in range(8):
                cast_dma(W0b[:, k, 0:512], w_in0[k * 128:(k + 1) * 128, 512:1024], writes=[("W0bQ", k)])
                cast_dma(W0b[:, k, 512:1536], w_in0[k * 128:(k + 1) * 128, 2048:3072], writes=[("W0bG", k)])

        wm.reset()
        lamb = wm.alloc([256], F32)
        lprod = wm.alloc([256], F32)
        lsum = wm.alloc([4], F32)
        dma(lamb, lam_d.partition_broadcast(128), writes=["lamb"])
        S.op("dve", lambda e: e.tensor_tensor(out=lprod[:, 0:64], in0=lamb[:, 0:64], in1=lamb[:, 64:128], op=ALU.mult),
             reads=["lamb"], writes=["lprod0"])
        S.op("dve", lambda e: e.tensor_tensor(out=lprod[:, 64:128], in0=lamb[:, 128:192], in1=lamb[:, 192:256], op=ALU.mult),
             reads=["lamb"], writes=["lprod1"])
        S.op("dve", lambda e: e.tensor_reduce(out=lsum[:, 0:2], in_=lprod[:, 0:128].rearrange("p (a b) -> p a b", a=2),
                                              axis=AX.X, op=ALU.add), reads=["lprod0", "lprod1"], writes=["lsum"])
        S.op("act", lambda e: e.activation(out=lsum[:, 2:4], in_=lsum[:, 0:2], func=AF.Exp), reads=["lsum"], writes=["lexp"])
        S.op("dve", lambda e: e.tensor_tensor(out=neglam, in0=lsum[:, 3:4], in1=lsum[:, 2:3], op=ALU.subtract),
             reads=["lexp"], writes=["neglam"])
        S.op("dve", lambda e: e.tensor_scalar(out=neglam, in0=neglam, scalar1=-LAM_INIT, scalar2=None, op0=ALU.add),
             reads=["neglam"], writes=["neglam"])

        ccs = wm.alloc([8, 2], F32)
        scs = wm.alloc([8, 2], F32)
        dma(ccs, cc_d, writes=["ccs"])
        S.op("act", lambda e: e.activation(out=scs, in_=ccs, func=AF.Silu), reads=["ccs"], writes=["scs"])
        AW = [wm.alloc([8, 1024], BF16) for _ in range(6)]
        scs_bf = wm.alloc([8, 2], BF16)
        S.op("dve", lambda e: e.tensor_copy(out=scs_bf, in_=scs), reads=["scs"], writes=["scs_bf"])
        diag = wm.alloc([8, 128], F32)
        awi = 0
        for l in range(2):
            psA = PS[l]
            for third in range(3):
                slot = awi
                awi += 1
                cast_dma(AW[slot], ada_w[l][:, third * 1024:(third + 1) * 1024].rearrange("(k p) c -> p k c", p=128),
                         writes=[("AW", slot)])
                for j in range(8):
                    col = (third * 8 + j) * 2
                    for k in range(8):
                        S.op("pe", lambda e, o=psA[:, col:col + 2], a=AW[slot][:, k, j * 128:(j + 1) * 128], b=scs_bf[:, k, :], k=k:
                             e.matmul(o, lhsT=a, rhs=b, start=(k == 0), stop=(k == 7)),
                             reads=[("AW", slot), "scs_bf"], writes=[("ps", l)])
            S.op("dve", lambda e, l=l, psA=psA: e.tensor_tensor(
                out=mod[l], in0=psA[:, 0:48].rearrange("p (a b) -> p a b", b=2),
                in1=adab[:, l, :].unsqueeze(2).to_broadcast([128, 24, 2]), op=ALU.add),
                reads=[("ps", l), "adab"], writes=[("mod", l)])
            for v, (gi, si) in ([(0, (0, 1)), (1, (2, 3))] if l == 0 else [(0, (4, 5))]):
                S.op("dve", lambda e, l=l, v=v, gi=gi: e.scalar_tensor_tensor(
                    out=prm[:, gi, :], in0=mod[l][:, 8:16, v], scalar=1.0, in1=normg[:, l, :], op0=ALU.add, op1=ALU.mult),
                    reads=[("mod", l), "normg"], writes=[("prm", gi)])
                S.op("dve", lambda e, l=l, v=v, si=si: e.tensor_copy(out=prm[:, si, :], in_=mod[l][:, 0:8, v]),
                     reads=[("mod", l)], writes=[("prm", si)])
            for j in range(8):
                S.op("dve", lambda e, l=l, j=j: e.tensor_scalar(out=diag[:, j, :], in0=ident32, scalar1=mod[l][:, 16 + j, 0:1],
                                                                 scalar2=None, op0=ALU.mult),
                     reads=[("mod", l), "ident32"], writes=[("diag", j)])
                S.op("pe", lambda e, l=l, j=j: e.matmul(PS[2 + j // 4][:, (j % 4) * 128:(j % 4 + 1) * 128], lhsT=ones32, rhs=diag[:, j, :],
                                                       start=True, stop=True),
                     reads=[("diag", j), "ones32"], writes=[("ps", 2 + j // 4)])
            for hf in range(2):
                S.op("act", lambda e, l=l, hf=hf: e.activation(out=gate_bc[l][:, hf * 512:(hf + 1) * 512], in_=PS[2 + hf], func=AF.Copy),
                     reads=[("ps", 2 + hf)], writes=[("gate_bc", l, hf)])
        emit_w0_loads()
        if debug:
            dbgt = wm.alloc([256], F32)
            S.op("dve", lambda e: e.tensor_copy(out=dbgt[:, 0:48], in_=prm.rearrange("p a b -> p (a b)")),
                 reads=[("prm", i) for i in range(6)], writes=["dbgt0"])
            S.op("dve", lambda e: e.tensor_copy(out=dbgt[:, 48:49], in_=neglam), reads=["neglam"], writes=["dbgt1"])
            S.op("dve", lambda e: e.tensor_copy(out=dbgt[:, 64:192], in_=gate_bc[1][:, 512:640]), reads=[("gate_bc", 1, 1)], writes=["dbgt2"])
            dma(dbg_d, dbgt, reads=["dbgt0", "dbgt1", "dbgt2"])
        S.barrier(tiny)
        if stop_after == "P0":
            S.emit()
            return nc

        def norm_pre(xin, xres, nt, xs, xsres, junk, ss):
            for t in range(nt):
                S.op("act", lambda e, t=t: e.activation(out=junk[:, t % 2, :], in_=xin[:, t, :], func=AF.Square, accum_out=ss[:, t:t + 1]),
                     reads=xres(t), writes=[("ss", t), ("junk", t % 2)])
            S.op("dve", lambda e: e.tensor_scalar(out=ss[:, 4:4 + nt], in0=ss[:, 0:nt], scalar1=1.0 / D, scalar2=EPS,
                                                  op0=ALU.mult, op1=ALU.add),
                 reads=[("ss", t) for t in range(nt)], writes=["ss_t"])
            S.op("pool", lambda e: e.tensor_tensor(out=ss[:, 8:8 + nt], in0=ss[:, 4:4 + nt], in1=nhalf.to_broadcast([128, nt]), op=ALU.pow),
                 reads=["ss_t", "nhalf"], writes=["rstd"])
            for t in range(nt):
                if t % 2 == 0:
                    S.op("dve", lambda e, t=t: e.tensor_scalar(out=xs[:, t, :], in0=xin[:, t, :], scalar1=ss[:, 8 + t:9 + t], scalar2=None,
                                                               op0=ALU.mult), reads=xres(t) + ["rstd"], writes=[(xsres, t)])
                else:
                    S.op("act", lambda e, t=t: e.activation(out=xs[:, t, :], in_=xin[:, t, :], func=AF.Copy, scale=ss[:, 8 + t:9 + t]),
                         reads=xres(t) + ["rstd"], writes=[(xsres, t)])

        def norm_T(nt, gi, si, hT, hres, xs, xsres):
            n = nt * 128
            for bank in range(4):
                for fc in (2 * bank, 2 * bank + 1):
                    for t in range(nt):
                        S.op("pe", lambda e, fc=fc, t=t, bank=bank: e.transpose(
                            out=PSB[bank][:, (fc % 2) * 512 + t * 128:(fc % 2) * 512 + (t + 1) * 128],
                            in_=xs[:, t, fc * 128:(fc + 1) * 128], identity=ident),
                            reads=[(xsres, t), "ident"], writes=[("ps", bank)])
                for fc in (2 * bank, 2 * bank + 1):
                    if bank % 2 == 0:
                        S.op("act", lambda e, fc=fc, bank=bank: e.activation(
                            out=hT[:, fc, 0:n], in_=PSB[bank][:, (fc % 2) * 512:(fc % 2) * 512 + n], func=AF.Identity,
                            scale=prm[:, gi, fc:fc + 1], bias=prm[:, si, fc:fc + 1]),
                            reads=[("ps", bank), ("prm", gi), ("prm", si)], writes=[(hres, fc)])
                    else:
                        S.op("dve", lambda e, fc=fc, bank=bank: e.tensor_scalar(
                            out=hT[:, fc, 0:n], in0=PSB[bank][:, (fc % 2) * 512:(fc % 2) * 512 + n],
                            scalar1=prm[:, gi, fc:fc + 1], scalar2=prm[:, si, fc:fc + 1], op0=ALU.mult, op1=ALU.add),
                            reads=[("ps", bank), ("prm", gi), ("prm", si)], writes=[(hres, fc)])

        def qk_post(ps, psres, g_bc, gres, rope, roperes, outb, outres, tmp, tr):
            sq, xg, A, B, st8 = tmp
            g3 = lambda ap: ap.rearrange("p (g d) -> p g d", g=8)
            g4 = lambda ap: ap.rearrange("p (g h s) -> p g h s", g=16, h=2)
            S.op("act", lambda e: e.activation(out=sq, in_=ps, func=AF.Square), reads=[psres], writes=[(tr, "sq")])
            S.op("dve", lambda e: e.tensor_tensor(out=g3(xg), in0=g3(ps), in1=g_bc.unsqueeze(1).to_broadcast([128, 8, 64]), op=ALU.mult),
                 reads=[psres, gres], writes=[(tr, "xg")])
            S.op("dve", lambda e: e.tensor_reduce(out=st8[:, 0:8], in_=g3(sq), axis=AX.X, op=ALU.add),
                 reads=[(tr, "sq")], writes=[(tr, "ssg")])
            S.op("dve", lambda e: e.tensor_scalar(out=st8[:, 8:16], in0=st8[:, 0:8], scalar1=1.0 / 64, scalar2=EPS, op0=ALU.mult, op1=ALU.add),
                 reads=[(tr, "ssg")], writes=[(tr, "t8")])
            S.op("pool", lambda e: e.tensor_tensor(out=st8[:, 16:24], in0=st8[:, 8:16], in1=nhalf.to_broadcast([128, 8]), op=ALU.pow),
                 reads=[(tr, "t8"), "nhalf"], writes=[(tr, "r8")])
            TC = rope[:, 0:64]
            TS = rope[:, 64:128]
            S.op("dve", lambda e: e.tensor_tensor(out=g3(A), in0=g3(xg), in1=TC.unsqueeze(1).to_broadcast([128, 8, 64]), op=ALU.mult),
                 reads=[(tr, "xg"), roperes], writes=[(tr, "A")])
            TS4 = TS.rearrange("p (r h s) -> p r h s", r=2, h=2)
            for hh in range(2):
                xv = xg.rearrange("p (g r h s) -> p g r h s", g=8, r=2, h=2)[:, :, :, 1 - hh, :]
                bv = B.rearrange("p (g r h s) -> p g r h s", g=8, r=2, h=2)[:, :, :, hh, :]
                tv = TS4[:, :, hh, :].unsqueeze(1).to_broadcast([128, 8, 2, 16])
                S.op("dve" if hh == 0 else "pool", lambda e, xv=xv, bv=bv, tv=tv: e.tensor_tensor(out=bv, in0=xv, in1=tv, op=ALU.mult),
                     reads=[(tr, "xg"), roperes], writes=[(tr, "B", hh)])
            S.op("pool", lambda e: e.tensor_tensor(out=A, in0=A, in1=B, op=ALU.add),
                 reads=[(tr, "A"), (tr, "B", 0), (tr, "B", 1)], writes=[(tr, "A")])
            S.op("dve", lambda e: e.tensor_tensor(out=g3(outb), in0=g3(A), in1=st8[:, 16:24].unsqueeze(2).to_broadcast([128, 8, 64]), op=ALU.mult),
                 reads=[(tr, "A"), (tr, "r8")], writes=[outres])

        def load_w(dst, src_cols, name):
            for k in range(8):
                cast_dma(dst[:, k, :], src_cols[k * 128:(k + 1) * 128, :], writes=[(name, k)])
            return [(name, k) for k in range(8)]

        wm.reset()
        W0a_res = [("W0aF", k) for k in range(8)] + [("W0aKV", k) for k in range(8)]
        xin_b = [wm.alloc([4, 1024], F32) for _ in range(2)]
        xs_b = [wm.alloc([4, 1024], BF16) for _ in range(2)]
        hT_b = [wm.alloc([8, 512], BF16) for _ in range(2)]
        junk = wm.alloc([2, 1024], BF16)
        ss_b = [wm.alloc([12], F32) for _ in range(2)]
        rope_b = [wm.alloc([4, 128], F32) for _ in range(3)]
        qtmp = [(wm.alloc([512], F32), wm.alloc([512], F32), wm.alloc([512], F32), wm.alloc([512], F32), wm.alloc([24], F32))
                for _ in range(2)]
        kb_b = [wm.alloc([512], BF16) for _ in range(3)]
        KTst = [wm.alloc([4, 512], BF16) for _ in range(2)]
        Fst = [wm.alloc([4, 512], BF16) for _ in range(2)]
        Vst = [wm.alloc([4, 512], BF16) for _ in range(2)]
        pbank = Rot([4, 5, 6, 7, 0, 1, 2, 3])
        qrot = Rot([0, 1])
        kbrot = Rot([0, 1, 2])

        def proj_tok(hT, hres_l, t, W, wcols, wres_l, bank):
            for fc in range(8):
                S.op("pe", lambda e, fc=fc: e.matmul(PS[bank], lhsT=hT[:, fc, t * 128:(t + 1) * 128], rhs=W[:, fc, wcols[0]:wcols[1]],
                                                     start=(fc == 0), stop=(fc == 7)),
                     reads=[hres_l[fc]] + wres_l, writes=[("ps", bank)])

        def qk_tile(bank, g_bc, gres, rope_t, roperes, KT_dst, ktres, t):
            qi = qrot.next()
            ki = kbrot.next()
            kb = kb_b[ki]
            qk_post(PS[bank], ("ps", bank), g_bc, gres, rope_t, roperes, kb, ("kb", ki), qtmp[qi], ("qt", qi))

            def part_b():
                tb = pbank.next()
                for h in range(4):
                    S.op("pe", lambda e, h=h: e.transpose(out=PSB[tb][:, h * 128:(h + 1) * 128], in_=kb[:, h * 128:(h + 1) * 128], identity=ident),
                         reads=[("kb", ki), "ident"], writes=[("ps", tb)])
                S.op("act", lambda e: e.activation(out=KT_dst[:, :, t * 128:(t + 1) * 128],
                                                   in_=PSB[tb][:, 0:512].rearrange("p (h n) -> p h n", h=4), func=AF.Copy),
                     reads=[("ps", tb)], writes=[(ktres, t)])
            return part_b

        nblk_a = 1 + NTOK // 512

        def blk_a(blk):
            if blk == 0:
                return 2, ctx_d.rearrange("(t p) d -> p t d", p=128), 0, 2, 3
            return (4, x_all[(blk - 1) * 512:blk * 512, :].rearrange("(t p) d -> p t d", p=128), NCTX + (blk - 1) * 512, 0, 1)

        def load_a(blk):
            nt, src, key0, gi, si = blk_a(blk)
            sl = blk % 2
            dma(xin_b[sl][:, 0:nt, :], src, writes=[("xin", sl)])
            dma(rope_b[blk % 3][:, 0:nt, :], ropek_d[key0:key0 + nt * 128, :].rearrange("(t p) c -> p t c", p=128), writes=[("rope", blk % 3)])

        xin_res = lambda sl: (lambda t: [("xin", sl)])

        def pre_a(blk):
            nt = blk_a(blk)[0]
            sl = blk % 2
            norm_pre(xin_b[sl], xin_res(sl), nt, xs_b[sl], ("xs", sl), junk, ss_b[sl])

        load_a(0)
        load_a(1)
        pre_a(0)
        deferred = []
        for blk in range(nblk_a):
            sl = blk % 2
            nt, src, key0, gi, si = blk_a(blk)
            n = nt * 128
            if blk + 1 < nblk_a:
                pre_a(blk + 1)
            if blk + 2 < nblk_a:
                load_a(blk + 2)
            if blk == 0:
                norm_T(nt, gi, si, hT_b[sl], ("hT", sl), xs_b[sl], ("xs", sl))
            hres_l = [(("hT", sl), fc) for fc in range(8)]
            for t in range(nt):
                if t == nt - 1 and blk + 1 < nblk_a:
                    ntn, _, _, gin, sin_ = blk_a(blk + 1)
                    norm_T(ntn, gin, sin_, hT_b[1 - sl], ("hT", 1 - sl), xs_b[1 - sl], ("xs", 1 - sl))
                bank = pbank.next()
                proj_tok(hT_b[sl], hres_l, t, W0a, (512, 1024), W0a_res, bank)
                part_b = qk_tile(bank, kg_bc, "kg", rope_b[blk % 3][:, t, :], ("rope", blk % 3), KTst[sl], ("KTst", sl), t)
                bank = pbank.next()
                proj_tok(hT_b[sl], hres_l, t, W0a, (1024, 1536), W0a_res, bank)
                S.op("dve", lambda e, bank=bank, t=t: e.tensor_copy(out=Vst[sl][:, t, :], in_=PS[bank]),
                     reads=[("ps", bank)], writes=[("Vst", sl, t)])
                if blk > 0:
                    bank = pbank.next()
                    proj_tok(hT_b[sl], hres_l, t, W0a, (0, 512), W0a_res, bank)
                    S.op("act", lambda e, bank=bank, t=t: e.activation(out=Fst[sl][:, t, :], in_=PS[bank], func=AF.Copy),
                         reads=[("ps", bank)], writes=[("Fst", sl, t)])
                while deferred:
                    deferred.pop(0)()
                deferred.append(part_b)
            deferred.append(lambda sl=sl, key0=key0, n=n, nt=nt: dma(KT_scr[:, :, key0:key0 + n], KTst[sl][:, :, 0:n],
                                                                  reads=[(("KTst", sl), t) for t in range(nt)]))
            dma(V_scr[key0:key0 + n, :].rearrange("(t p) c -> p t c", p=128), Vst[sl][:, 0:nt, :],
                reads=[("Vst", sl, t) for t in range(nt)])
            if blk > 0:
                dma(F_scr[(blk - 1) * 512:blk * 512, :].rearrange("(t p) c -> p t c", p=128), Fst[sl],
                    reads=[("Fst", sl, t) for t in range(4)])
        while deferred:
            deferred.pop(0)()
        S.barrier(tiny)
        if stop_after == "P1a":
            S.emit()
            return nc

        own_blocks = [(i * 512, 4) for i in range(8)] + [(4096, 1)]
        wm.reset()
        W0b_res = [("W0bQ", k) for k in range(8)] + [("W0bG", k) for k in range(8)]
        xin_b = [wm.alloc([4, 1024], F32) for _ in range(2)]
        xs_b = [wm.alloc([4, 1024], BF16) for _ in range(2)]
        hT_b = [wm.alloc([8, 512], BF16) for _ in range(2)]
        junk = wm.alloc([2, 1024], BF16)
        ss_b = [wm.alloc([12], F32) for _ in range(2)]
        rope_b = [wm.alloc([4, 128], F32) for _ in range(3)]
        qtmp = [(wm.alloc([512], F32), wm.alloc([512], F32), wm.alloc([512], F32), wm.alloc([512], F32), wm.alloc([24], F32))
                for _ in range(2)]
        kb_b = [wm.alloc([512], BF16) for _ in range(5)]
        kbrot = Rot([0, 1, 2, 3, 4])
        QTst = [wm.alloc([4, 512], BF16) for _ in range(2)]
        GTst = [wm.alloc([8, 512], BF16) for _ in range(2)]
        def load_b(bi):
            tok0, nt = own_blocks[bi]
            sl = bi % 2
            n = nt * 128
            dma(xin_b[sl][:, 0:nt, :], x_own[tok0:tok0 + n, :].rearrange("(t p) d -> p t d", p=128), writes=[("xin", sl)])
            dma(rope_b[bi % 3][:, 0:nt, :], ropeq_d[tok0:tok0 + n, :].rearrange("(t p) c -> p t c", p=128), writes=[("rope", bi % 3)])

        def pre_b(bi):
            sl = bi % 2
            norm_pre(xin_b[sl], xin_res(sl), own_blocks[bi][1], xs_b[sl], ("xs", sl), junk, ss_b[sl])

        NB = len(own_blocks)
        load_b(0)
        load_b(1)
        pre_b(0)
        for bi, (tok0, nt) in enumerate(own_blocks):
            sl = bi % 2
            n = nt * 128
            if bi + 1 < NB:
                pre_b(bi + 1)
            if bi + 2 < NB:
                load_b(bi + 2)
            if bi == 0:
                norm_T(nt, 0, 1, hT_b[sl], ("hT", sl), xs_b[sl], ("xs", sl))
            hres_l = [(("hT", sl), fc) for fc in range(8)]
            defs = []
            for t in range(nt):
                bank = pbank.next()
                proj_tok(hT_b[sl], hres_l, t, W0b, (0, 512), W0b_res, bank)
                defs.append(qk_tile(bank, qg_bc, "qg", rope_b[bi % 3][:, t, :], ("rope", bi % 3), QTst[sl], ("QTst", sl), t))
            for gc in range(8):
                if gc == 5 and bi + 1 < NB:
                    norm_T(own_blocks[bi + 1][1], 0, 1, hT_b[1 - sl], ("hT", 1 - sl), xs_b[1 - sl], ("xs", 1 - sl))
                if gc >= 4 and defs:
                    defs.pop(0)()
                bank = pbank.next()
                for fc in range(8):
                    S.op("pe", lambda e, fc=fc, gc=gc, bank=bank: e.matmul(
                        PS[bank][:, 0:n], lhsT=W0b[:, fc, 512 + gc * 128:512 + (gc + 1) * 128], rhs=hT_b[sl][:, fc, 0:n],
                        start=(fc == 0), stop=(fc == 7)), reads=[hres_l[fc]] + W0b_res, writes=[("ps", bank)])
                S.op("act", lambda e, gc=gc, bank=bank: e.activation(out=GTst[sl][:, gc, 0:n], in_=PS[bank][:, 0:n], func=AF.Silu),
                     reads=[("ps", bank)], writes=[("GTst", sl, gc)])
            while defs:
                defs.pop(0)()
            dma(QT_scr[:, :, tok0:tok0 + n], QTst[sl][:, :, 0:n], reads=[(("QTst", sl), t) for t in range(nt)])
            dma(GT_scr[:, :, tok0:tok0 + n], GTst[sl][:, :, 0:n], reads=[("GTst", sl, gc) for gc in range(8)])
        S.barrier(tiny)
        if stop_after == "P1b":
            S.emit()
            return nc

        wm.limit = SBUF_BYTES
        wm.reset()
        Fs = wm.alloc([64, 512], BF16)
        T1 = wm.alloc([64, 2, 128], BF16)
        Zst = [wm.alloc([2, 512], BF16) for _ in range(3)]
        Fv = F_scr.rearrange("(p a) c -> p a c", a=64)
        for q4 in range(4):
            dma(Fs[:, q4 * 16:(q4 + 1) * 16, :], Fv[:, q4 * 16:(q4 + 1) * 16, :], writes=[("Fs", q4)])
            dma(T1[:, q4 * 16:(q4 + 1) * 16, :, :], t1_d[:, q4 * 16:(q4 + 1) * 16, :, :], writes=[("T1", q4)])
        zrot = Rot([0, 1, 2])
        prot = Rot(list(range(8)))
        for a in range(64):
            zi = zrot.next()
            for part in range(2):
                bank = prot.next()
                S.op("pe", lambda e, a=a, part=part, bank=bank: e.matmul(PS[bank], lhsT=T1[:, a, part, :], rhs=Fs[:, a, :], start=True, stop=True),
                     reads=[("Fs", a // 16), ("T1", a // 16)], writes=[("ps", bank)])
                if part == 0:
                    S.op("act", lambda e, bank=bank, zi=zi: e.activation(out=Zst[zi][:, 0, :], in_=PS[bank], func=AF.Copy),
                         reads=[("ps", bank)], writes=[("Zst", zi, 0)])
                else:
                    S.op("dve", lambda e, bank=bank, zi=zi: e.tensor_copy(out=Zst[zi][:, 1, :], in_=PS[bank]),
                         reads=[("ps", bank)], writes=[("Zst", zi, 1)])
            dma(Z_scr[:, a, :, :].rearrange("q k c -> k q c"), Zst[zi], reads=[("Zst", zi, 0), ("Zst", zi, 1)])
        S.barrier(tiny)
        wm.reset()
        Xs = wm.alloc([4, 2, 33 * 128], BF16)
        Zs = [wm.alloc([4, 512], BF16) for _ in range(3)]
        FOst = [wm.alloc([4, 512], BF16) for _ in range(2)]
        Zv = Z_scr.rearrange("q a k c -> (q a) k c")
        zrot = Rot([0, 1, 2])
        ei = 0
        for kc in range(32):
            zi = zrot.next()
            dma(Zs[zi], Zv[:, kc * 4:(kc + 1) * 4, :], writes=[("Zs", zi)])
            for g in range(4):
                bank = prot.next()
                for j in range(4):
                    S.op("pe", lambda e, g=g, j=j, bank=bank, zi=zi: e.matmul(PS[bank][:, j * 66:(j + 1) * 66], lhsT=Zs[zi][:, j, g * 128:(g + 1) * 128],
                                                                       rhs=t2, start=True, stop=True),
                         reads=[("Zs", zi), "t2"], writes=[("ps", bank)])
                src = PS[bank][:, 0:264].rearrange("p (j q a) -> p q a j", j=4, q=2)
                dst = Xs[:, g, :, :].rearrange("p q (a k) -> p q a k", k=128)[:, :, :, kc * 4:(kc + 1) * 4]
                eng = "act" if (ei % 2 == 0) else "dve"
                ei += 1
                for part in range(2):
                    if eng == "act":
                        S.op("act", lambda e, s=src[:, part], d=dst[:, part]: e.activation(out=d, in_=s, func=AF.Copy),
                             reads=[("ps", bank)], writes=[("Xs", g, kc, part)])
                    else:
                        S.op("dve", lambda e, s=src[:, part], d=dst[:, part]: e.tensor_copy(out=d, in_=s),
                             reads=[("ps", bank)], writes=[("Xs", g, kc, part)])
        xs_all = [("Xs", g, kc, part) for g in range(4) for kc in range(32) for part in range(2)]
        for bi, (tok0, nt) in enumerate(own_blocks):
            sl = bi % 2
            n = nt * 128
            for g in range(4):
                bank = prot.next()
                for part in range(2):
                    S.op("pe", lambda e, g=g, part=part, bank=bank: e.matmul(PS[bank][:, 0:n], lhsT=ccsc[:, part, :], rhs=Xs[:, g, part, tok0:tok0 + n],
                                                                      start=(part == 0), stop=(part == 1)),
                         reads=(xs_all if (bi == 0 and g == 0 and part == 0) else []) + ["ccsc"], writes=[("ps", bank)])
                if g % 2 == 0:
                    S.op("act", lambda e, g=g, bank=bank: e.activation(out=FOst[sl][:, g, 0:n], in_=PS[bank][:, 0:n], func=AF.Copy),
                         reads=[("ps", bank)], writes=[("FOst", sl, g)])
                else:
                    S.op("dve", lambda e, g=g, bank=bank: e.tensor_copy(out=FOst[sl][:, g, 0:n], in_=PS[bank][:, 0:n]),
                         reads=[("ps", bank)], writes=[("FOst", sl, g)])
            dma(FO_scr[:, :, tok0:tok0 + n], FOst[sl][:, :, 0:n], reads=[("FOst", sl, g) for g in range(4)])
        S.barrier(tiny)
        if stop_after == "P2":
            S.emit()
            return nc

        wm.reset()
        KT = wm.alloc([4, NKEY], BF16)
        Vs = wm.alloc([NKT, 512], BF16)
        Qb = [wm.alloc([4, 512], BF16) for _ in range(2)]
        NPT = 6
        Ptb = [wm.alloc([2, 512], BF16) for _ in range(NPT)]
        Pt = [[Ptb[i][:, m, :] for i in range(NPT)] for m in range(2)]
        accb = [wm.alloc([2, 512], F32) for _ in range(2)]
        acc = [[accb[p][:, m, :] for m in range(2)] for p in range(2)]
        P2b = [wm.alloc([2, 512], BF16) for _ in range(2)]
        P4b = [wm.alloc([2, 512], BF16) for _ in range(2)]
        osb = [wm.alloc([512], F32) for _ in range(2)]
        fr = [wm.alloc([512], F32) for _ in range(3)]
        fsq = wm.alloc([512], BF16)
        AOst = [wm.alloc([4, 512], BF16) for _ in range(2)]
        Vv = V_scr.rearrange("(t p) c -> p t c", p=128)
        dma(KT[:, 0, :], KT_scr[:, 0, :], writes=[("KT", 0)])
        for q6 in range(6):
            dma(Vs[:, q6 * 11:(q6 + 1) * 11, :], Vv[:, q6 * 11:(q6 + 1) * 11, :], writes=[("Vs", q6)])
        for h in range(1, 4):
            dma(KT[:, h, :], KT_scr[:, h, :], writes=[("KT", h)], q="pool")
        prt = Rot(list(range(NPT)))
        pending = []
        hcount = 0

        def make_finalize(h, p, n, bsl, tok0, last_head):
            OB = [4 + 2 * p, 5 + 2 * p]

            def s1():
                for m in range(2):
                    S.op("dve", lambda e: e.tensor_copy(out=osb[m][:, 0:n], in_=PS[OB[m]][:, 0:n]), reads=[("ps", OB[m])], writes=[("osb", m)])
                S.op("pe", lambda e: e.matmul(PS[OB[0]][:, 0:n], lhsT=ones32, rhs=acc[p][0][:, 0:n], start=True, stop=True),
                     reads=[("acc", p, 0), "ones32"], writes=[("ps", OB[0])])
                S.op("pe", lambda e: e.matmul(PS[OB[1]][:, 0:n], lhsT=ones32, rhs=acc[p][1][:, 0:n], start=True, stop=True),
                     reads=[("acc", p, 1), "ones32"], writes=[("ps", OB[1])])

            def s2m(m):
                def f():
                    S.op("dve", lambda e: e.reciprocal(out=fr[m][:, 0:n], in_=PS[OB[m]][:, 0:n]), reads=[("ps", OB[m])], writes=[("fr", m)])
                    S.op("dve", lambda e: e.tensor_tensor(out=osb[m][:, 0:n], in0=osb[m][:, 0:n], in1=fr[m][:, 0:n], op=ALU.mult),
                         reads=[("osb", m), ("fr", m)], writes=[("osb", m)])
                return f

            def s2c():
                S.op("dve", lambda e: e.scalar_tensor_tensor(out=osb[0][:, 0:n], in0=osb[1][:, 0:n], scalar=neglam, in1=osb[0][:, 0:n],
                                                             op0=ALU.mult, op1=ALU.add), reads=[("osb", 0), ("osb", 1), "neglam"], writes=[("osb", 0)])

            def s3():
                S.op("act", lambda e: e.activation(out=fsq[:, 0:n], in_=osb[0][:, 0:n], func=AF.Square), reads=[("osb", 0)], writes=["fsq"])

            def s4():
                S.op("pe", lambda e: e.matmul(PS[OB[0]][:, 0:n], lhsT=onesm, rhs=fsq[:, 0:n], start=True, stop=True),
                     reads=["fsq", "onesm"], writes=[("ps", OB[0])])

            def s5():
                S.op("act", lambda e: e.activation(out=fr[2][:, 0:n], in_=PS[OB[0]][:, 0:n], func=AF.Ln, bias=epsv), reads=[("ps", OB[0]), "epsv"],
                     writes=[("fr", 2)])
                S.op("act", lambda e: e.activation(out=fr[2][:, 0:n], in_=fr[2][:, 0:n], func=AF.Exp, scale=-0.5), reads=[("fr", 2)], writes=[("fr", 2)])

            def s6():
                S.op("dve", lambda e: e.scalar_tensor_tensor(out=AOst[bsl][:, h, 0:n], in0=osb[0][:, 0:n], scalar=subln8, in1=fr[2][:, 0:n],
                                                             op0=ALU.mult, op1=ALU.mult),
                     reads=[("osb", 0), ("fr", 2), "subln8"], writes=[("AOst", bsl, h)])
                if last_head:
                    dma(AO_scr[:, :, tok0:tok0 + n], AOst[bsl][:, :, 0:n], reads=[("AOst", bsl, hh) for hh in range(4)])
            return [(1, s1), (4, s2m(0)), (9, s2m(1)), (13, s2c), (17, s3), (19, s4), (21, s5), (22, s6)]

        def attn_head(bi, tok0, nt, h, p, nxt, qk0_done):
            sl = bi % 2
            n = nt * 128
            OB = [4 + 2 * p, 5 + 2 * p]

            def qk(kt):
                s2_ = kt % 2
                for m in range(2):
                    bank = s2_ * 2 + m
                    S.op("pe", lambda e: e.matmul(PS[bank][:, 0:n], lhsT=KT[m * 64:(m + 1) * 64, h, kt * 128:(kt + 1) * 128],
                                                  rhs=Qb[sl][m * 64:(m + 1) * 64, h, 0:n], start=True, stop=True),
                         reads=[("KT", h), ("Qb", sl)], writes=[("ps", bank)])

            def pv(kt, pi):
                for m in range(2):
                    S.op("pe", lambda e: e.matmul(PS[OB[m]][:, 0:n], lhsT=Vs[:, kt, h * 128:(h + 1) * 128], rhs=Pt[m][pi][:, 0:n],
                                                  start=(kt == 0), stop=(kt == NKT - 1)),
                         reads=[("Pt", m, pi), ("Vs", kt // 11)], writes=[("ps", OB[m])])

            if not qk0_done:
                qk(0)
            pi_prev = None
            for kt in range(NKT):
                s2_ = kt % 2
                pi = prt.next()
                S.op("act", lambda e: e.activation(out=Ptb[pi][:, :, 0:n],
                                                   in_=PSALL[:, s2_ * 1024:(s2_ + 1) * 1024].rearrange("p (m q) -> p m q", m=2)[:, :, 0:n],
                                                   func=AF.Exp, scale=0.125),
                     reads=[("ps", s2_ * 2), ("ps", s2_ * 2 + 1)], writes=[("Pt", 0, pi), ("Pt", 1, pi)])
                if kt + 1 < NKT:
                    qk(kt + 1)
                elif nxt is not None:
                    nbi, nnt, nh = nxt
                    for m in range(2):
                        S.op("pe", lambda e: e.matmul(PS[m][:, 0:nnt * 128], lhsT=KT[m * 64:(m + 1) * 64, nh, 0:128],
                                                      rhs=Qb[nbi % 2][m * 64:(m + 1) * 64, nh, 0:nnt * 128], start=True, stop=True),
                             reads=[("KT", nh), ("Qb", nbi % 2)], writes=[("ps", m)])
                if kt >= 1:
                    pv(kt - 1, pi_prev)
                if kt % 2 == 1:
                    r2 = (kt // 2) % 2
                    S.op("dve", lambda e: e.tensor_tensor(out=P2b[r2][:, :, 0:n], in0=Ptb[pi_prev][:, :, 0:n], in1=Ptb[pi][:, :, 0:n], op=ALU.add),
                         reads=[("Pt", 0, pi_prev), ("Pt", 1, pi_prev), ("Pt", 0, pi), ("Pt", 1, pi)], writes=[("P2", r2)])
                    src = None
                    if kt % 4 == 3:
                        r4 = (kt // 4) % 2
                        S.op("dve", lambda e: e.tensor_tensor(out=P4b[r4][:, :, 0:n], in0=P2b[0][:, :, 0:n], in1=P2b[1][:, :, 0:n], op=ALU.add),
                             reads=[("P2", 0), ("P2", 1)], writes=[("P4", r4)])
                        src, sres = P4b[r4], ("P4", r4)
                    elif kt == NKT - 1:
                        src, sres = P2b[r2], ("P2", r2)
                    if src is not None:
                        if kt == 3:
                            S.op("dve", lambda e: e.tensor_copy(out=accb[p][:, :, 0:n], in_=src[:, :, 0:n]),
                                 reads=[sres], writes=[("acc", p, 0), ("acc", p, 1)])
                        else:
                            S.op("dve", lambda e: e.tensor_tensor(out=accb[p][:, :, 0:n], in0=accb[p][:, :, 0:n], in1=src[:, :, 0:n], op=ALU.add),
                                 reads=[sres, ("acc", p, 0), ("acc", p, 1)], writes=[("acc", p, 0), ("acc", p, 1)])
                pi_prev = pi
                while pending and pending[0][0] <= kt:
                    pending.pop(0)[1]()
            pv(NKT - 1, pi_prev)

        dma(Qb[0][:, :, 0:512], QT_scr[:, :, 0:512], writes=[("Qb", 0)])
        for bi, (tok0, nt) in enumerate(own_blocks):
            if bi + 1 < len(own_blocks):
                t0n, ntn = own_blocks[bi + 1]
                dma(Qb[1 - bi % 2][:, :, 0:ntn * 128], QT_scr[:, :, t0n:t0n + ntn * 128], writes=[("Qb", 1 - bi % 2)])
            for h in range(4):
                p = hcount % 2
                hcount += 1
                if h < 3:
                    nxt = (bi, nt, h + 1)
                elif bi + 1 < len(own_blocks):
                    nxt = (bi + 1, own_blocks[bi + 1][1], 0)
                else:
                    nxt = None
                attn_head(bi, tok0, nt, h, p, nxt, hcount > 1)
                assert not pending
                pending.extend(make_finalize(h, p, nt * 128, bi % 2, tok0, h == 3))
        while pending:
            pending.pop(0)[1]()
        S.barrier(tiny)
        if stop_after == "P3":
            S.emit()
            return nc

        wm.reset()
        Wo0 = wm.alloc([8, 1024], BF16)
        W1a = wm.alloc([8, 2048], BF16)
        Wo0_res = load_w(Wo0, w_out0, "Wo0")
        W1a_res = load_w(W1a, w_in1[:, 1024:3072], "W1a")
        FAb = wm.alloc([8, 512], BF16)
        GTb = wm.alloc([8, 512], BF16)
        yT = wm.alloc([8, 512], BF16)
        xin_b = [wm.alloc([4, 1024], F32) for _ in range(2)]
        x1b = [wm.alloc([4, 1024], F32) for _ in range(2)]
        tmpb = [wm.alloc([512], F32) for _ in range(2)]
        xs1 = wm.alloc([4, 1024], BF16)
        h1T = wm.alloc([8, 512], BF16)
        junk = wm.alloc([2, 1024], BF16)
        ss1 = wm.alloc([12], F32)
        cgs = [wm.alloc([512], F32) for _ in range(2)]
        ust = [wm.alloc([8, 512], BF16) for _ in range(2)]
        zcol = wm.alloc([8, 1], BF16)
        S.op("pool", lambda e: e.memset(zcol, 0.0), writes=["zcol"])
        trot = Rot([0, 1])
        crot = Rot([0, 1])

        def load_fg(bi):
            tok0, nt = own_blocks[bi]
            n = nt * 128
            dma(FAb[:, 0:4, 0:n], FO_scr[:, :, tok0:tok0 + n], writes=[("FAb", 0)])
            dma(FAb[:, 4:8, 0:n], AO_scr[:, :, tok0:tok0 + n], writes=[("FAb", 1)])
            dma(GTb[:, :, 0:n], GT_scr[:, :, tok0:tok0 + n], writes=["GTb"])

        def load_x(bi):
            tok0, nt = own_blocks[bi]
            dma(xin_b[bi % 2][:, 0:nt, :], x_own[tok0:tok0 + nt * 128, :].rearrange("(t p) d -> p t d", p=128), writes=[("xin", bi % 2)])

        def outproj0(bi):
            tok0, nt = own_blocks[bi]
            sl = bi % 2
            n = nt * 128
            S.op("dve", lambda e: e.tensor_tensor(out=yT[:, 0:4, 0:n], in0=FAb[:, 0:4, 0:n], in1=GTb[:, 0:4, 0:n], op=ALU.mult),
                 reads=[("FAb", 0), "GTb"], writes=[("yT", 0)])
            S.op("pool", lambda e: e.tensor_tensor(out=yT[:, 4:8, 0:n], in0=FAb[:, 4:8, 0:n], in1=GTb[:, 4:8, 0:n], op=ALU.mult),
                 reads=[("FAb", 1), "GTb"], writes=[("yT", 1)])
            if bi + 1 < NB:
                load_fg(bi + 1)
            for t in range(nt):
                for hf in range(2):
                    bank = pbank.next()
                    for fc in range(8):
                        S.op("pe", lambda e: e.matmul(PS[bank], lhsT=yT[:, fc, t * 128:(t + 1) * 128], rhs=Wo0[:, fc, hf * 512:(hf + 1) * 512],
                                                      start=(fc == 0), stop=(fc == 7)), reads=[("yT", fc // 4)] + Wo0_res, writes=[("ps", bank)])
                    ti = trot.next()
                    S.op("dve", lambda e: e.tensor_tensor(out=tmpb[ti], in0=PS[bank], in1=gate_bc[0][:, hf * 512:(hf + 1) * 512], op=ALU.mult),
                         reads=[("ps", bank)], writes=[("tmpb", ti)])
                    S.op("pool", lambda e: e.tensor_tensor(out=x1b[sl][:, t, hf * 512:(hf + 1) * 512], in0=tmpb[ti],
                                                           in1=xin_b[sl][:, t, hf * 512:(hf + 1) * 512], op=ALU.add),
                         reads=[("tmpb", ti), ("xin", sl)], writes=[("x1b", sl, t, hf)])
            dma(X1_scr[tok0:tok0 + n, :].rearrange("(t p) d -> p t d", p=128), x1b[sl][:, 0:nt, :],
                reads=[("x1b", sl, t, hf) for t in range(nt) for hf in range(2)])

        load_fg(0)
        load_x(0)
        load_x(1)
        outproj0(0)
        for bi, (tok0, nt) in enumerate(own_blocks):
            sl = bi % 2
            n = nt * 128
            norm_pre(x1b[sl], (lambda t, sl=sl: [("x1b", sl, t, 0), ("x1b", sl, t, 1)]), nt, xs1, "xs1", junk, ss1)
            if bi + 1 < NB:
                outproj0(bi + 1)
            if bi + 2 < NB:
                load_x(bi + 2)
            norm_T(nt, 4, 5, h1T, "h1T", xs1, "xs1")
            h1res = [("h1T", fc) for fc in range(8)]
            for cc in range(8):
                b1 = pbank.next()
                b2 = pbank.next()
                for which, bank in ((0, b1), (1, b2)):
                    for fc in range(8):
                        S.op("pe", lambda e: e.matmul(PS[bank][:, 0:n], lhsT=W1a[:, fc, which * 1024 + cc * 128:which * 1024 + (cc + 1) * 128],
                                                      rhs=h1T[:, fc, 0:n], start=(fc == 0), stop=(fc == 7)),
                             reads=[h1res[fc]] + W1a_res, writes=[("ps", bank)])
                ci = crot.next()
                S.op("act", lambda e: e.activation(out=cgs[ci][:, 0:n], in_=PS[b1][:, 0:n], func=AF.Copy),
                     reads=[("ps", b1)], writes=[("cgs", ci)])
                S.op("dve", lambda e: e.tensor_tensor(out=ust[sl][:, cc, 0:n], in0=cgs[ci][:, 0:n], in1=PS[b2][:, 0:n], op=ALU.mult),
                     reads=[("ps", b2), ("cgs", ci)], writes=[("ust", sl, cc)])
            dma(UT_scr[:, :, 1 + tok0:1 + tok0 + n], ust[sl][:, :, 0:n], reads=[("ust", sl, cc) for cc in range(8)])
        dma(UT_scr[:, :, 0:1], zcol, reads=["zcol"], allow_slow_non_contiguous=True)
        dma(UT_scr[:, :, NOWN + 1:NOWN + 2], zcol, reads=["zcol"], allow_slow_non_contiguous=True)
        S.barrier(tiny)
        if stop_after == "P4a":
            S.emit()
            return nc

        wm.reset()
        W1b = wm.alloc([8, 2048], BF16)
        Wo1 = wm.alloc([8, 1024], BF16)
        W1b_res = load_w(W1b[:, :, 0:1024], w_in1[:, 0:1024], "W1bB") + load_w(W1b[:, :, 1024:2048], w_in1[:, 3072:4096], "W1bG")
        Wo1_res = load_w(Wo1, w_out1, "Wo1")
        ub = [wm.alloc([8, 514], BF16) for _ in range(2)]
        xin_b = [wm.alloc([4, 1024], F32) for _ in range(3)]
        x2b = wm.alloc([4, 1024], F32)
        tmpb = [wm.alloc([512], F32) for _ in range(2)]
        xs1 = [wm.alloc([4, 1024], BF16) for _ in range(2)]
        h1T = wm.alloc([8, 512], BF16)
        junk = wm.alloc([2, 1024], BF16)
        ss1 = [wm.alloc([12], F32) for _ in range(2)]
        sgb = [wm.alloc([512], F32) for _ in range(2)]
        vb = [wm.alloc([512], F32) for _ in range(2)]
        cvb = [wm.alloc([512], F32) for _ in range(2)]
        y1T = wm.alloc([8, 512], BF16)

        def load_x1(bi):
            tok0, nt = own_blocks[bi]
            dma(xin_b[bi % 3][:, 0:nt, :], X1_scr[tok0:tok0 + nt * 128, :].rearrange("(t p) d -> p t d", p=128), writes=[("xin", bi % 3)])

        def load_u(bi):
            tok0, nt = own_blocks[bi]
            dma(ub[bi % 2][:, :, 0:nt * 128 + 2], UT_scr[:, :, tok0:tok0 + nt * 128 + 2], writes=[("ub", bi % 2)])

        def pre_c(bi):
            norm_pre(xin_b[bi % 3], (lambda t, k=bi % 3: [("xin", k)]), own_blocks[bi][1], xs1[bi % 2], ("xs1", bi % 2), junk, ss1[bi % 2])

        load_x1(0)
        load_x1(1)
        load_u(0)
        pre_c(0)
        for bi, (tok0, nt) in enumerate(own_blocks):
            sl = bi % 2
            x3 = bi % 3
            n = nt * 128
            if bi + 1 < NB:
                pre_c(bi + 1)
            if bi + 2 < NB:
                load_x1(bi + 2)
            if bi + 1 < NB:
                load_u(bi + 1)
            norm_T(nt, 4, 5, h1T, "h1T", xs1[sl], ("xs1", sl))
            h1res = [("h1T", fc) for fc in range(8)]
            for cc in range(8):
                b1 = pbank.next()
                b2 = pbank.next()
                for which, bank in ((0, b1), (1, b2)):
                    for fc in range(8):
                        S.op("pe", lambda e: e.matmul(PS[bank][:, 0:n], lhsT=W1b[:, fc, which * 1024 + cc * 128:which * 1024 + (cc + 1) * 128],
                                                      rhs=h1T[:, fc, 0:n], start=(fc == 0), stop=(fc == 7)),
                             reads=[h1res[fc]] + W1b_res, writes=[("ps", bank)])
                ci = crot.next()
                S.op("act", lambda e: e.activation(out=sgb[ci][:, 0:n], in_=PS[b2][:, 0:n], func=AF.Silu),
                     reads=[("ps", b2)], writes=[("sgb", ci)])
                S.op("dve", lambda e: e.tensor_tensor(out=vb[ci][:, 0:n], in0=PS[b1][:, 0:n], in1=sgb[ci][:, 0:n], op=ALU.mult),
                     reads=[("ps", b1), ("sgb", ci)], writes=[("vb", ci)])
                S.op("dve", lambda e: e.tensor_scalar(out=cvb[ci][:, 0:n], in0=ub[sl][:, cc, 1:n + 1], scalar1=conv_sb[:, cc, 1:2],
                                                      scalar2=None, op0=ALU.mult), reads=[("ub", sl), "conv"], writes=[("cvb", ci)])
                S.op("dve", lambda e: e.scalar_tensor_tensor(out=cvb[ci][:, 0:n], in0=ub[sl][:, cc, 0:n], scalar=conv_sb[:, cc, 0:1],
                                                             in1=cvb[ci][:, 0:n], op0=ALU.mult, op1=ALU.add),
                     reads=[("ub", sl), "conv", ("cvb", ci)], writes=[("cvb", ci)])
                S.op("dve", lambda e: e.scalar_tensor_tensor(out=cvb[ci][:, 0:n], in0=ub[sl][:, cc, 2:n + 2], scalar=conv_sb[:, cc, 2:3],
                                                             in1=cvb[ci][:, 0:n], op0=ALU.mult, op1=ALU.add),
                     reads=[("ub", sl), "conv", ("cvb", ci)], writes=[("cvb", ci)])
                S.op("pool", lambda e: e.tensor_tensor(out=y1T[:, cc, 0:n], in0=cvb[ci][:, 0:n], in1=vb[ci][:, 0:n], op=ALU.mult),
                     reads=[("cvb", ci), ("vb", ci)], writes=[("y1T", cc)])
            for t in range(nt):
                for hf in range(2):
                    bank = pbank.next()
                    for fc in range(8):
                        S.op("pe", lambda e: e.matmul(PS[bank], lhsT=y1T[:, fc, t * 128:(t + 1) * 128], rhs=Wo1[:, fc, hf * 512:(hf + 1) * 512],
                                                      start=(fc == 0), stop=(fc == 7)), reads=[("y1T", fc)] + Wo1_res, writes=[("ps", bank)])
                    ti = trot.next()
                    S.op("dve", lambda e: e.tensor_tensor(out=tmpb[ti], in0=PS[bank], in1=gate_bc[1][:, hf * 512:(hf + 1) * 512], op=ALU.mult),
                         reads=[("ps", bank)], writes=[("tmpb", ti)])
                    S.op("pool", lambda e: e.tensor_tensor(out=x2b[:, t, hf * 512:(hf + 1) * 512], in0=tmpb[ti],
                                                           in1=xin_b[x3][:, t, hf * 512:(hf + 1) * 512], op=ALU.add),
                         reads=[("tmpb", ti), ("xin", x3)], writes=[("x2b", t, hf)])
            dma(out_d[tok0:tok0 + n, :].rearrange("(t p) d -> p t d", p=128), x2b[:, 0:nt, :],
                reads=[("x2b", t, hf) for t in range(nt) for hf in range(2)])
        S.emit()
    return nc


def _rope_tables(pos_r, pos_c):
    half = 32
    freqs = 10000.0 ** (-np.arange(0, half, 2, dtype=np.float64) / half)
    ar = pos_r[:, None] * freqs
    ac = pos_c[:, None] * freqs
    n = pos_r.shape[0]
    TC = np.concatenate([np.cos(ar), np.cos(ar), np.cos(ac), np.cos(ac)], axis=1)
    TS = np.concatenate([-np.sin(ar), np.sin(ar), -np.sin(ac), np.sin(ac)], axis=1)
    return np.concatenate([TC, TS], axis=1).astype(np.float32)


def _consts():
    bf = ml_dtypes.bfloat16
    pos = np.arange(NTOK)
    rope_lat = _rope_tables((pos // 64).astype(np.float64), (pos % 64).astype(np.float64))
    rope_ctx = np.zeros((NCTX, 128), np.float32)
    rope_ctx[:, 0:64] = 1.0
    ropek = np.concatenate([rope_ctx, rope_lat], axis=0)
    p = np.arange(128, dtype=np.float64)[:, None, None]
    a = np.arange(64, dtype=np.float64)[None, :, None]
    kp = np.arange(128, dtype=np.float64)[None, None, :]
    th = 2 * np.pi * (p * kp / 128.0 + a * kp / 8192.0)
    t1 = np.stack([np.cos(th), -np.sin(th)], axis=2).astype(bf)
    c = np.arange(128, dtype=np.float64)
    thc = 2 * np.pi * np.outer(c, c) / 128.0
    ccsc = np.stack([np.cos(thc) / 1024.0, np.sin(thc) / 1024.0], axis=1).astype(bf)
    t2s = []
    for s in range(2):
        ka = (31 * s + np.arange(33)).astype(np.float64)
        aa = np.arange(64, dtype=np.float64)
        th2 = 2 * np.pi * np.outer(aa, ka) / 64.0
        top = np.concatenate([np.cos(th2), -np.sin(th2)], axis=1)
        bot = np.concatenate([np.sin(th2), np.cos(th2)], axis=1)
        t2s.append(np.concatenate([top, bot], axis=0).astype(bf))
    ident = np.eye(128, dtype=np.float32).astype(bf)
    return rope_lat, ropek, t1, ccsc, t2s, ident


def make_in_maps(x, c, ctx, c_ctx, norm_g, ada_w, ada_b, even_w_in, even_q_norm, even_k_norm,
                 even_lambda_q1, even_lambda_k1, even_lambda_q2, even_lambda_k2, even_subln,
                 even_w_out, odd_w_in, odd_conv_w, odd_w_out):
    f = lambda a: np.ascontiguousarray(np.asarray(a, dtype=np.float32))
    x, c, ctx, c_ctx, norm_g, ada_w, ada_b = map(f, (x, c, ctx, c_ctx, norm_g, ada_w, ada_b))
    rope_lat, ropek, t1, ccsc, t2s, ident = _consts()
    w_in0 = f(even_w_in)[0]
    w_out0 = f(even_w_out)[0]
    w_in1 = f(odd_w_in)[0]
    w_out1 = f(odd_w_out)[0]
    adab_l = f(ada_b).reshape(2, 24, 128).transpose(2, 0, 1)
    normg_l = f(norm_g).reshape(2, 8, 128).transpose(2, 0, 1)
    qkg = np.stack([f(even_q_norm)[0], f(even_k_norm)[0]], axis=0)
    lam = np.concatenate([f(even_lambda_q1)[0], f(even_lambda_k1)[0], f(even_lambda_q2)[0], f(even_lambda_k2)[0]])
    subln = f(even_subln)[0].reshape(128, 1)
    conv = f(odd_conv_w)[0].reshape(3, 8, 128).transpose(2, 1, 0)
    cctx_l = c_ctx.reshape(8, 128).T
    maps = []
    for core in range(8):
        b, s = core // 2, core % 2
        q0 = 3968 * s
        cc = np.stack([c[b].reshape(8, 128).T, cctx_l], axis=2)
        maps.append({
            "x_all": x[b], "x_own": np.ascontiguousarray(x[b, q0:q0 + NOWN]), "ctx": ctx[b],
            "cc": np.ascontiguousarray(cc), "ada_w": ada_w, "ada_b": np.ascontiguousarray(adab_l),
            "norm_g": np.ascontiguousarray(normg_l), "w_in0": w_in0, "w_out0": w_out0, "w_in1": w_in1, "w_out1": w_out1,
            "qk_g": np.ascontiguousarray(qkg), "lam": np.ascontiguousarray(lam), "subln": np.ascontiguousarray(subln),
            "conv_w": np.ascontiguousarray(conv), "ropek": ropek, "ropeq": np.ascontiguousarray(rope_lat[q0:q0 + NOWN]),
            "t1": t1, "t2": t2s[s], "ccsc": ccsc, "ident": ident,
        })
    return maps


_NC_CACHE = {}


def kernel(**inputs):
    maps = make_in_maps(**inputs)
    if "nc" not in _NC_CACHE:
        _NC_CACHE["nc"] = build()
    res = run_bass_kernel_spmd(_NC_CACHE["nc"], maps, core_ids=list(range(8)))
    out = np.empty((4, NTOK, D), np.float32)
    for core in range(8):
        b, s = core // 2, core % 2
        o = res.results[core]["out"]
        if s == 0:
            out[b, 0:4096] = o[0:4096]
        else:
            out[b, 4096:8192] = o[128:NOWN]
    return out
```
